# Optimizing a Trainium2 kernel written in Bass

```python
import math
import jax, jax.numpy as jnp
from jax import lax
import numpy as np

D_MODEL = 1024
BATCH = 16
SEQ = 2048
DEPTH = 2

SSM_GROUP = 16
SSM_GROUPS = D_MODEL // 64
SSM_WIDTH = SSM_GROUPS * SSM_GROUP
SSM_STATE = 64
DIFF_HEADS = D_MODEL // 256
DIFF_HEAD_DIM = 64
DIFF_WIDTH = DIFF_HEADS * 2 * DIFF_HEAD_DIM
SWA_Q_HEADS = D_MODEL // 256
SWA_KV_HEADS = 2
SWA_GROUP = SWA_Q_HEADS // SWA_KV_HEADS
SWA_HEAD_DIM = 64
SWA_WIDTH = SWA_Q_HEADS * SWA_HEAD_DIM
SWA_KV_WIDTH = SWA_KV_HEADS * SWA_HEAD_DIM
WINDOW = 128
Q_BLOCK = 128
D_MIX = SSM_WIDTH + DIFF_WIDTH + SWA_WIDTH
IN_COLS = 2 * SSM_WIDTH + 4 * DIFF_WIDTH + 2 * SWA_WIDTH + 2 * SWA_KV_WIDTH
N_ATTN_HEADS = DIFF_HEADS + SWA_Q_HEADS
RMS_EPS = 1e-6

kernel_name = "hybrid_s5_diffattn_swa_adaln"


def rms_norm(x, gain):
    xf = x.astype(jnp.float32)
    y = xf * lax.rsqrt(jnp.mean(xf * xf, axis=-1, keepdims=True) + RMS_EPS)
    return (y * gain.astype(jnp.float32)).astype(x.dtype)


def alibi_slopes():
    i = jnp.arange(1, N_ATTN_HEADS + 1, dtype=jnp.float32)
    return jnp.exp2(-i * (8.0 / N_ATTN_HEADS))


def split_cols(proj):
    sizes = [SSM_WIDTH, SSM_WIDTH,
             DIFF_WIDTH, DIFF_WIDTH, DIFF_WIDTH, DIFF_WIDTH,
             SWA_WIDTH, SWA_KV_WIDTH, SWA_KV_WIDTH, SWA_WIDTH]
    out, start = [], 0
    for n in sizes:
        out.append(proj[..., start:start + n])
        start += n
    return out


def s5_mixer(u, lam_re, lam_im, log_step, b_re, b_im, c_re, c_im, d_skip, w_glu, b_glu):
    bsz, seq, _ = u.shape
    ug = u.reshape(bsz, seq, SSM_GROUPS, SSM_GROUP).astype(jnp.float32)
    step = jnp.exp(log_step.astype(jnp.float32))[:, None]
    lr = lam_re.astype(jnp.float32)
    li = lam_im.astype(jnp.float32)
    mag = jnp.exp(lr * step)
    ang = li * step
    ab_re = mag * jnp.cos(ang)
    ab_im = mag * jnp.sin(ang)
    den = lr * lr + li * li
    f_re = ((ab_re - 1.0) * lr + ab_im * li) / den
    f_im = (ab_im * lr - (ab_re - 1.0) * li) / den
    br = b_re.astype(jnp.float32)
    bi = b_im.astype(jnp.float32)
    bb_re = f_re[..., None] * br - f_im[..., None] * bi
    bb_im = f_re[..., None] * bi + f_im[..., None] * br
    bu_re = jnp.einsum('bsgh,gph->bsgp', ug, bb_re)
    bu_im = jnp.einsum('bsgh,gph->bsgp', ug, bb_im)
    a_re = jnp.broadcast_to(ab_re, bu_re.shape)
    a_im = jnp.broadcast_to(ab_im, bu_im.shape)

    def combine(e1, e2):
        a1r, a1i, b1r, b1i = e1
        a2r, a2i, b2r, b2i = e2
        return (a2r * a1r - a2i * a1i,
                a2r * a1i + a2i * a1r,
                a2r * b1r - a2i * b1i + b2r,
                a2r * b1i + a2i * b1r + b2i)

    _, _, st_re, st_im = lax.associative_scan(combine, (a_re, a_im, bu_re, bu_im), axis=1)
    y = (jnp.einsum('bsgp,ghp->bsgh', st_re, c_re.astype(jnp.float32))
         - jnp.einsum('bsgp,ghp->bsgh', st_im, c_im.astype(jnp.float32))
         + d_skip.astype(jnp.float32).reshape(SSM_GROUPS, SSM_GROUP) * ug)
    y = jax.nn.gelu(y.reshape(bsz, seq, SSM_WIDTH))
    y = y * jax.nn.sigmoid(y @ w_glu.astype(jnp.float32) + b_glu.astype(jnp.float32))
    return y.astype(u.dtype)


def diff_attention(q, k, v, slopes, lam, subln_gain, lam_init):
    bsz, seq = q.shape[0], q.shape[1]
    nb = seq // Q_BLOCK
    scale = DIFF_HEAD_DIM ** -0.5
    k1, k2 = k[:, :, :, 0], k[:, :, :, 1]

    def to_blocks(t):
        return t.reshape(bsz, nb, Q_BLOCK, DIFF_HEADS, DIFF_HEAD_DIM).transpose(1, 0, 2, 3, 4)

    q1b, q2b = to_blocks(q[:, :, :, 0]), to_blocks(q[:, :, :, 1])
    key_pos = jnp.arange(seq)

    def block(args):
        q1_blk, q2_blk, blk = args
        t = blk * Q_BLOCK + jnp.arange(Q_BLOCK)
        dist = t[:, None] - key_pos[None, :]
        mask = dist >= 0
        bias = -slopes[:, None, None] * dist.astype(jnp.float32)[None]

        def probs(qb, kk):
            s = jnp.einsum('bqhd,bkhd->bhqk', qb, kk).astype(jnp.float32) * scale + bias
            return jax.nn.softmax(jnp.where(mask, s, -jnp.inf), axis=-1)

        p = probs(q1_blk, k1) - lam * probs(q2_blk, k2)
        return jnp.einsum('bhqk,bkhd->bqhd', p.astype(v.dtype), v)

    o = lax.map(block, (q1b, q2b, jnp.arange(nb)))
    o = o.transpose(1, 0, 2, 3, 4).reshape(bsz, seq, DIFF_HEADS, 2 * DIFF_HEAD_DIM)
    o = rms_norm(o, subln_gain) * (1.0 - lam_init)
    return o.reshape(bsz, seq, DIFF_WIDTH)


def sliding_window_attention(q, k, v, slopes, sinks):
    bsz, seq = q.shape[0], q.shape[1]
    nb = seq // WINDOW
    scale = SWA_HEAD_DIM ** -0.5
    qb = q.reshape(bsz, nb, WINDOW, SWA_KV_HEADS, SWA_GROUP, SWA_HEAD_DIM)

    def band(t):
        cur = t.reshape(bsz, nb, WINDOW, SWA_KV_HEADS, SWA_HEAD_DIM)
        prev = jnp.concatenate([jnp.zeros_like(cur[:, :1]), cur[:, :-1]], axis=1)
        return jnp.concatenate([prev, cur], axis=2)

    kk, vv = band(k), band(v)
    t_loc = WINDOW + jnp.arange(WINDOW)
    s_loc = jnp.arange(2 * WINDOW)
    dist = t_loc[:, None] - s_loc[None, :]
    valid = (jnp.arange(nb)[:, None] * WINDOW - WINDOW + s_loc[None, :]) >= 0
    mask = ((dist >= 0) & (dist < WINDOW))[None] & valid[:, None, :]
    sl = slopes.reshape(SWA_KV_HEADS, SWA_GROUP)
    bias = -sl[:, :, None, None] * dist.astype(jnp.float32)
    s = jnp.einsum('bnqkgd,bnskd->bnkgqs', qb, kk).astype(jnp.float32) * scale + bias[None, None]
    s = jnp.where(mask[None, :, None, None], s, -jnp.inf)
    sink = sinks.astype(jnp.float32).reshape(SWA_KV_HEADS, SWA_GROUP)[None, None, :, :, None, None]
    m = jnp.maximum(jnp.max(s, axis=-1, keepdims=True), sink)
    e = jnp.exp(s - m)
    p = e / (jnp.sum(e, axis=-1, keepdims=True) + jnp.exp(sink - m))
    o = jnp.einsum('bnkgqs,bnskd->bnqkgd', p.astype(v.dtype), vv)
    return o.reshape(bsz, seq, SWA_WIDTH)


def hybrid_layer(x, cond, layer_idx, norm_gain, ada_w, ada_b, w_in, w_out,
                 ssm_lam_re, ssm_lam_im, ssm_log_step, ssm_b_re, ssm_b_im,
                 ssm_c_re, ssm_c_im, ssm_d, glu_w, glu_b,
                 diff_lq1, diff_lk1, diff_lq2, diff_lk2, diff_subln, swa_sinks):
    bsz, seq, _ = x.shape
    mod = cond @ ada_w + ada_b
    shift, scl, gate = jnp.split(mod, 3, axis=-1)
    h = rms_norm(x, norm_gain) * (1.0 + scl[:, None, :]) + shift[:, None, :]
    proj = h @ w_in
    (u, z_ssm, qd, kd, vd, z_diff, qs, ks, vs, z_swa) = split_cols(proj)

    slopes = alibi_slopes()
    swa_slopes = slopes[:SWA_Q_HEADS]
    diff_slopes = slopes[SWA_Q_HEADS:]

    y_ssm = s5_mixer(u, ssm_lam_re, ssm_lam_im, ssm_log_step, ssm_b_re, ssm_b_im,
                     ssm_c_re, ssm_c_im, ssm_d, glu_w, glu_b) * jax.nn.silu(z_ssm)

    lam_init = 0.8 - 0.6 * math.exp(-0.3 * layer_idx)
    lam = (jnp.exp(jnp.sum(diff_lq1.astype(jnp.float32) * diff_lk1.astype(jnp.float32)))
           - jnp.exp(jnp.sum(diff_lq2.astype(jnp.float32) * diff_lk2.astype(jnp.float32)))
           + lam_init)
    qd = qd.reshape(bsz, seq, DIFF_HEADS, 2, DIFF_HEAD_DIM)
    kd = kd.reshape(bsz, seq, DIFF_HEADS, 2, DIFF_HEAD_DIM)
    vd = vd.reshape(bsz, seq, DIFF_HEADS, 2 * DIFF_HEAD_DIM)
    y_diff = diff_attention(qd, kd, vd, diff_slopes, lam, diff_subln, lam_init) * jax.nn.silu(z_diff)

    qs = qs.reshape(bsz, seq, SWA_Q_HEADS, SWA_HEAD_DIM)
    ks = ks.reshape(bsz, seq, SWA_KV_HEADS, SWA_HEAD_DIM)
    vs = vs.reshape(bsz, seq, SWA_KV_HEADS, SWA_HEAD_DIM)
    y_swa = sliding_window_attention(qs, ks, vs, swa_slopes, swa_sinks) * jax.nn.silu(z_swa)

    y = jnp.concatenate([y_ssm, y_diff.astype(x.dtype), y_swa.astype(x.dtype)], axis=-1) @ w_out
    return x + gate[:, None, :] * y


def setup_inputs(seed: int = 0) -> dict:
    key = jax.random.key(seed)
    ks = jax.random.split(key, 26)
    f32 = jnp.float32

    def nrm(k, shape, scale):
        return jax.random.normal(k, shape, f32) * scale

    L, G, P, H = DEPTH, SSM_GROUPS, SSM_STATE, SSM_GROUP
    ada_b = nrm(ks[4], (L, 3 * D_MODEL), 0.01)
    ada_b = ada_b.at[:, 2 * D_MODEL:].add(1.0)
    lam_im = jnp.broadcast_to(jnp.pi * jnp.arange(P, dtype=f32), (L, G, P)) + nrm(ks[8], (L, G, P), 0.01)
    return {
        "x": nrm(ks[0], (BATCH, SEQ, D_MODEL), 1.0),
        "c": nrm(ks[1], (BATCH, D_MODEL), 1.0),
        "norm_gain": 1.0 + nrm(ks[2], (L, D_MODEL), 0.02),
        "ada_w": nrm(ks[3], (L, D_MODEL, 3 * D_MODEL), 0.1 * D_MODEL ** -0.5),
        "ada_b": ada_b,
        "w_in": nrm(ks[5], (L, D_MODEL, IN_COLS), D_MODEL ** -0.5),
        "w_out": nrm(ks[6], (L, D_MIX, D_MODEL), D_MIX ** -0.5),
        "ssm_lam_re": -0.5 + nrm(ks[7], (L, G, P), 0.01),
        "ssm_lam_im": lam_im,
        "ssm_log_step": jax.random.uniform(ks[9], (L, G), f32, math.log(1e-3), math.log(1e-1)),
        "ssm_b_re": nrm(ks[10], (L, G, P, H), (2 * H) ** -0.5),
        "ssm_b_im": nrm(ks[11], (L, G, P, H), (2 * H) ** -0.5),
        "ssm_c_re": nrm(ks[12], (L, G, H, P), 0.5),
        "ssm_c_im": nrm(ks[13], (L, G, H, P), 0.5),
        "ssm_d": nrm(ks[14], (L, SSM_WIDTH), 1.0),
        "glu_w": nrm(ks[15], (L, SSM_WIDTH, SSM_WIDTH), SSM_WIDTH ** -0.5),
        "glu_b": nrm(ks[16], (L, SSM_WIDTH), 0.01),
        "diff_lq1": nrm(ks[17], (L, DIFF_HEAD_DIM), 0.1),
        "diff_lk1": nrm(ks[18], (L, DIFF_HEAD_DIM), 0.1),
        "diff_lq2": nrm(ks[19], (L, DIFF_HEAD_DIM), 0.1),
        "diff_lk2": nrm(ks[20], (L, DIFF_HEAD_DIM), 0.1),
        "diff_subln": 1.0 + nrm(ks[21], (L, 2 * DIFF_HEAD_DIM), 0.02),
        "swa_sinks": nrm(ks[22], (L, SWA_Q_HEADS), 1.0),
        "final_gain": 1.0 + nrm(ks[23], (D_MODEL,), 0.02),
    }


def reference(x, c, norm_gain, ada_w, ada_b, w_in, w_out, ssm_lam_re, ssm_lam_im,
              ssm_log_step, ssm_b_re, ssm_b_im, ssm_c_re, ssm_c_im, ssm_d, glu_w, glu_b,
              diff_lq1, diff_lk1, diff_lq2, diff_lk2, diff_subln, swa_sinks, final_gain):
    cond = jax.nn.silu(c)
    for l in range(DEPTH):
        x = hybrid_layer(x, cond, l, norm_gain[l], ada_w[l], ada_b[l], w_in[l], w_out[l],
                         ssm_lam_re[l], ssm_lam_im[l], ssm_log_step[l], ssm_b_re[l], ssm_b_im[l],
                         ssm_c_re[l], ssm_c_im[l], ssm_d[l], glu_w[l], glu_b[l],
                         diff_lq1[l], diff_lk1[l], diff_lq2[l], diff_lk2[l], diff_subln[l],
                         swa_sinks[l])
    return rms_norm(x, final_gain)
```

```python
import math
from contextlib import ExitStack

import numpy as np
import concourse.bass as bass
import concourse.mybir as mybir
from concourse.bass_utils import run_bass_kernel_spmd

F32 = mybir.dt.float32
BF16 = mybir.dt.bfloat16
I32 = mybir.dt.int32
ALU = mybir.AluOpType
AF = mybir.ActivationFunctionType
AX = mybir.AxisListType

D = 1024
L = 2
NCORES = 8
PHASES = ("ssm", "diff", "swa", "out")
SBUF_LEFT = []
CHECK_PSUM = False
PSUM_WARN = []
IN_COLS = 3328
EPS = 1e-6
TWO_PI = 2.0 * math.pi
SLOPES = [2.0 ** (-(i + 1)) for i in range(8)]


class Buf:
    __slots__ = ("name", "w", "r")

    def __init__(self, name):
        self.name = name
        self.w = None
        self.r = []


class Op:
    __slots__ = ("eng", "fn", "idx", "deps", "signal", "gend", "sem", "val", "waits")

    def __init__(self, eng, fn, idx, gend):
        self.eng = eng
        self.fn = fn
        self.idx = idx
        self.deps = []
        self.signal = False
        self.gend = gend
        self.sem = None
        self.val = 0
        self.waits = []


class Prog:
    ENGS = ("pe", "act", "dve", "pool", "sp")
    NDMA = 16

    def __init__(self):
        self.ops = {e: [] for e in self.ENGS}

    def add(self, eng, fn, reads=(), writes=(), gend=True):
        lst = self.ops[eng]
        op = Op(eng, fn, len(lst), gend)
        deps = []
        for b in reads:
            if CHECK_PSUM and b.name.startswith("ps") and b.name[2:3].isdigit():
                for r in b.r:
                    if r.eng != eng:
                        PSUM_WARN.append((b.name, r.eng, eng))
            if b.w is not None:
                deps.append(b.w)
        for b in writes:
            if b.w is not None:
                deps.append(b.w)
            deps.extend(b.r)
        seen = set()
        for d in deps:
            if d.eng == "pe":
                if eng == "pe":
                    continue
                pl = self.ops["pe"]
                j = d.idx
                while j < len(pl) and not pl[j].gend:
                    j += 1
                if j < len(pl):
                    d = pl[j]
            if id(d) in seen:
                continue
            seen.add(id(d))
            d.signal = True
            op.deps.append(d)
        lst.append(op)
        for b in writes:
            b.w = op
            b.r = []
        for b in reads:
            b.r.append(op)
        return op

    def resolve(self, sems, dsems):
        for e in self.ENGS:
            if e == "sp":
                cnt = [0] * self.NDMA
                for i, op in enumerate(self.ops[e]):
                    k = i % self.NDMA
                    cnt[k] += 16
                    op.signal = True
                    op.sem = dsems[k]
                    op.val = cnt[k]
                self.dma_final = [(dsems[k], cnt[k]) for k in range(self.NDMA) if cnt[k]]
            else:
                n = 0
                for op in self.ops[e]:
                    if op.signal:
                        n += 1
                        op.sem = sems[e]
                        op.val = n
        for e in self.ENGS:
            waited = {}
            for op in self.ops[e]:
                need = {}
                for d in op.deps:
                    k = id(d.sem)
                    if d.val > need.get(k, (None, 0))[1]:
                        need[k] = (d.sem, d.val)
                for k, (s, v) in need.items():
                    if waited.get(k, 0) >= v:
                        continue
                    waited[k] = v
                    op.waits.append((s, v))

    def emit(self, eng_name, e, final=False):
        for op in self.ops[eng_name]:
            if eng_name == "sp" and op.val > 16:
                e.wait_ge(op.sem, op.val - 16)
            for s, v in op.waits:
                e.wait_ge(s, v)
            ins = op.fn(e)
            if op.signal:
                ins.then_inc(op.sem, 16 if eng_name == "sp" else 1)
        if final:
            for s, v in self.dma_final:
                e.wait_ge(s, v)


def build_nc(NSEQ, S, dbg=None):
    NT = S // 128
    NB = S // 512
    nc = bass.Bass("TRN2", target_bir_lowering=False)
    P = Prog()

    def dram_in(name, shape):
        return nc.dram_tensor(name, list(shape), F32, kind="ExternalInput").ap()

    x_d = dram_in("x", [NSEQ, S, D])
    c_d = dram_in("c", [NSEQ, D])
    ng_d = dram_in("norm_gain", [L, D])
    adaw_d = dram_in("ada_w", [L, D, 3 * D])
    adab_d = dram_in("ada_b", [L, 3 * D])
    win_d = dram_in("w_in", [L, D, IN_COLS])
    wout_d = dram_in("w_out", [L, D, D])
    lamre_d = dram_in("ssm_lam_re", [L, 16, 64])
    lamim_d = dram_in("ssm_lam_im", [L, 16, 64])
    lstep_d = dram_in("ssm_log_step", [L, 16])
    bre_d = dram_in("ssm_b_re", [L, 16, 64, 16])
    bim_d = dram_in("ssm_b_im", [L, 16, 64, 16])
    cre_d = dram_in("ssm_c_re", [L, 16, 16, 64])
    cim_d = dram_in("ssm_c_im", [L, 16, 16, 64])
    ssmd_d = dram_in("ssm_d", [L, 256])
    gluw_d = dram_in("glu_w", [L, 256, 256])
    glub_d = dram_in("glu_b", [L, 256])
    lq1_d = dram_in("diff_lq1", [L, 64])
    lk1_d = dram_in("diff_lk1", [L, 64])
    lq2_d = dram_in("diff_lq2", [L, 64])
    lk2_d = dram_in("diff_lk2", [L, 64])
    subln_d = dram_in("diff_subln", [L, 128])
    sinks_d = dram_in("swa_sinks", [L, 4])
    fg_d = dram_in("final_gain", [D])
    out_d = nc.dram_tensor("out", [NSEQ, S, D], F32, kind="ExternalOutput").ap()
    modscr_d = nc.dram_tensor("modscr", [L, NSEQ, D], F32, kind="Internal").ap()
    dbg_d = None
    if dbg is not None:
        dbg_d = nc.dram_tensor("dbg", list(dbg), F32, kind="ExternalOutput").ap()

    es = ExitStack()
    bufs = {}

    def sb(name, shape, dt=F32):
        t = es.enter_context(nc.sbuf_tensor(name, list(shape), dt))
        bufs[name] = Buf(name)
        return t

    def B(name):
        if name not in bufs:
            bufs[name] = Buf(name)
        return bufs[name]

    xres = sb("xres", [128, NT, D])
    hT = sb("hT", [128, 8, S], BF16)
    mixT = sb("mixT", [128, 8, S], BF16)
    NFM = 3
    fm = [sb(f"fm{i}", [128, S], BF16) for i in range(NFM)]
    vtok = sb("vtok", [128, NT, 128], BF16)
    NSTG = 2
    wstage = [sb(f"wstage{i}", [128, 8, 128]) for i in range(NSTG)]
    NWB = 2
    wbf = [sb(f"wbf{i}", [128, 8, 128], BF16) for i in range(NWB)]
    if S >= 1024:
        wout_bf = None
    else:
        wout_bf = sb("wout_bf", [128, 8, D], BF16)
    NF = 8
    ftb = sb("ftb", [128, NF, 512])
    ft = [ftb[:, i, :] for i in range(NF)]
    xs_ap = ftb[:, 6:8, :].rearrange("p a n -> p (a n)")
    iraw = sb("iraw", [128, S], BF16)
    itile = sb("itile", [128, 1, 512], I32)
    NBT = 4
    bt = [sb(f"bt{i}", [128, 512], BF16) for i in range(NBT)]
    bc = sb("bc", [128, D])
    ssmw = sb("ssmw", [128, 8, 4, 128], BF16)
    gluw_sb = sb("gluw_sb", [128, 2, 256], BF16)
    ident_f = sb("ident_f", [128, 128])
    ones_f = sb("ones_f", [128, 128])
    ident_b = sb("ident_b", [128, 128], BF16)
    ones_b = sb("ones_b", [128, 128], BF16)
    rowmask = sb("rowmask", [128, 8])
    psw = sb("psw", [128, 128])
    sgn = sb("sgn", [128, 1])
    pidx = sb("pidx", [128, 19])
    biasd = sb("biasd", [128, 4, 19])
    mask_d = sb("mask_d", [128, 128], BF16)
    swab = sb("swab", [128, 4, 256], BF16)
    condT = sb("condT", [128, 8, NSEQ])
    modT = sb("modT", [128, L, 24, NSEQ])
    adabT = sb("adabT", [128, L, 24])
    ngT = sb("ngT", [128, L, 8])
    gmodT = sb("gmodT", [128, L, 8, NSEQ])
    small = sb("small", [128, 64])
    sp_a = sb("sp_a", [128, 16, 16])
    sp_b = sb("sp_b", [128, 16, 16])
    sp_c = sb("sp_c", [128, 16, 16])
    sp_d = sb("sp_d", [128, 16, 16])
    sp_s = sb("sp_s", [128, 20, 16])
    ssmdT = sb("ssmdT", [128, L, 2])
    glubT = sb("glubT", [128, L, 2])
    sublnT = sb("sublnT", [128, L])
    sinkc = sb("sinkc", [128, L, 2])
    frq = sb("frq", [128, 16])
    rho = sb("rho", [128, 16])
    carry = sb("carry", [128, 16])
    psb = [es.enter_context(nc.psum_tensor(f"ps{i}", [128, 512], F32)) for i in range(8)]
    for i in range(8):
        bufs[f"ps{i}"] = Buf(f"ps{i}")

    SBUF_LEFT.append(nc.sbuf_bytes_remaining)
    def dma(out_ap, in_ap, reads, writes, slow=False):
        def fn(e):
            if slow:
                return e.dma_start(out=out_ap, in_=in_ap, allow_slow_non_contiguous=True)
            return e.dma_start(out=out_ap, in_=in_ap)
        return P.add("sp", fn, [B(r) for r in reads], [B(w) for w in writes])

    def op(eng, fn, reads, writes, gend=True):
        return P.add(eng, fn, [B(r) for r in reads], [B(w) for w in writes], gend)

    def mm(out_ap, lhsT, rhs, reads, writes, start=True, stop=True):
        return op("pe", lambda e: e.matmul(out_ap, lhsT, rhs, start=start, stop=stop),
                  reads, writes, gend=stop)

    def tr(out_ap, in_ap, ident, reads, writes, gend=True):
        return op("pe", lambda e: e.transpose(out_ap, in_ap, ident), reads, writes, gend=gend)

    def act(out_ap, in_ap, func, reads, writes, bias=None, scale=None, accum=None):
        kw = {}
        if bias is not None:
            kw["bias"] = bias
        if scale is not None:
            kw["scale"] = scale
        if accum is not None:
            kw["accum_out"] = accum
        return op("act", lambda e: e.activation(out_ap, in_ap, func, **kw), reads, writes)

    def ts(eng, out_ap, in_ap, s1, s2, op0, op1, reads, writes):
        if s2 is None:
            return op(eng, lambda e: e.tensor_scalar(out_ap, in_ap, s1, None, op0), reads, writes)
        return op(eng, lambda e: e.tensor_scalar(out_ap, in_ap, s1, s2, op0, op1), reads, writes)

    def tt(eng, out_ap, a, b, o, reads, writes):
        return op(eng, lambda e: e.tensor_tensor(out_ap, a, b, o), reads, writes)

    def stt(out_ap, in0, scalar, in1, op0, op1, reads, writes):
        return op("dve", lambda e: e.scalar_tensor_tensor(out_ap, in0, scalar, in1, op0, op1),
                  reads, writes)

    def cp(eng, out_ap, in_ap, reads, writes):
        if eng == "act":
            return op("act", lambda e: e.copy(out_ap, in_ap), reads, writes)
        return op(eng, lambda e: e.tensor_copy(out_ap, in_ap), reads, writes)

    def memset(eng, ap, val, writes):
        return op(eng, lambda e: e.memset(ap, val), [], writes)

    def ftview(i):
        return ftb[:, i:i + 2, :].rearrange("p a n -> p (a n)"), [f"ft{i}", f"ft{i + 1}"]

    SC2PI = 6.283185
    HALFPI = 1.5707963

    def pts(out_ap, in_ap, s1, reads, writes, s2=0.0, op0=ALU.mult, op1=ALU.add):
        return op("pool", lambda e: e.tensor_scalar(out_ap, in_ap, s1, s2, op0, op1), reads, writes)

    def consts():
        memset("pool", ones_f[:], 1.0, ["ones_f"])
        memset("pool", ones_b[:], 1.0, ["ones_b"])
        op("pool", lambda e: e.affine_select(ident_f[:], ones_f[:], [[-1, 128]], ALU.is_equal, 0.0,
                                             base=0, channel_multiplier=1), ["ones_f"], ["ident_f"])
        cp("pool", ident_b[:], ident_f[:], ["ident_f"], ["ident_b"])
        op("pool", lambda e: e.affine_select(rowmask[:], ones_f[:, 0:8], [[-16, 8]], ALU.is_ge, 0.0,
                                             base=0, channel_multiplier=1), ["ones_f"], ["rowmask"])
        op("pool", lambda e: e.affine_select(rowmask[:], rowmask[:], [[16, 8]], ALU.is_ge, 0.0,
                                             base=15, channel_multiplier=-1), ["rowmask"], ["rowmask"])
        op("pool", lambda e: e.affine_select(sgn[:], ones_f[:, 0:1], [[0, 1]], ALU.is_ge, -1.0,
                                             base=-64, channel_multiplier=1), ["ones_f"], ["sgn"])
        op("pool", lambda e: e.iota(pidx[:], [[-128, 19]], base=384, channel_multiplier=1,
                                    allow_small_or_imprecise_dtypes=True), [], ["pidx"])
        for h in range(4):
            pts(biasd[:, h, :], pidx[:], SLOPES[4 + h], ["pidx"], ["biasd"])
        memset("pool", ft[0][:, 0:128], 0.0, ["ft0"])
        op("pool", lambda e: e.affine_select(ft[0][:, 0:128], ft[0][:, 0:128], [[1, 128]], ALU.is_ge,
                                             -240000.0, base=0, channel_multiplier=-1), ["ft0"], ["ft0"])
        cp("pool", mask_d[:], ft[0][:, 0:128], ["ft0"], ["mask_d"])
        for h in range(4):
            sl8 = 8.0 * SLOPES[h]
            op("pool", lambda e: e.iota(ft[1][:, 0:128], [[1, 128]], base=128, channel_multiplier=-1,
                                        allow_small_or_imprecise_dtypes=True), [], ["ft1"])
            pts(ft[1][:, 0:128], ft[1][:, 0:128], -sl8, ["ft1"], ["ft1"])
            op("pool", lambda e: e.affine_select(ft[1][:, 0:128], ft[1][:, 0:128], [[-1, 128]], ALU.is_ge,
                                                 -240000.0, base=-1, channel_multiplier=1), ["ft1"], ["ft1"])
            cp("pool", swab[:, h, 0:128], ft[1][:, 0:128], ["ft1"], ["swab"])
            op("pool", lambda e: e.iota(ft[1][:, 128:256], [[1, 128]], base=0, channel_multiplier=-1,
                                        allow_small_or_imprecise_dtypes=True), [], ["ft1"])
            pts(ft[1][:, 128:256], ft[1][:, 128:256], -sl8, ["ft1"], ["ft1"])
            op("pool", lambda e: e.affine_select(ft[1][:, 128:256], ft[1][:, 128:256], [[1, 128]], ALU.is_ge,
                                                 -240000.0, base=0, channel_multiplier=-1), ["ft1"], ["ft1"])
            cp("pool", swab[:, h, 128:256], ft[1][:, 128:256], ["ft1"], ["swab"])
        memset("pool", ssmw[:], 0.0, ["ssmw2", "ssmw3"] + [f"ssmw{k}_{g}" for k in range(2) for g in range(8)])
        op("pool", lambda e: e.affine_select(psw[:], ones_f[:], [[-1, 128]], ALU.is_equal, 0.0,
                                             base=-64, channel_multiplier=1), ["ones_f"], ["psw"])
        op("pool", lambda e: e.affine_select(ft[2][:, 0:128], ones_f[:], [[-1, 128]], ALU.is_equal, 0.0,
                                             base=64, channel_multiplier=1), ["ones_f"], ["ft2"])
        tt("pool", psw[:], psw[:], ft[2][:, 0:128], ALU.add, ["psw", "ft2"], ["psw"])

    def small_loads():
        for b in range(NSEQ):
            dma(condT[:, :, b], c_d[b].rearrange("(j p) -> p j", p=128), [], ["condT"], slow=True)
        dma(adabT[:], adab_d.rearrange("l (j p) -> p l j", p=128), [], ["adabT"], slow=True)
        dma(ngT[:], ng_d.rearrange("l (j p) -> p l j", p=128), [], ["ngT"], slow=True)
        dma(ssmdT[:], ssmd_d.rearrange("l (j p) -> p l j", p=128), [], ["ssmdT"], slow=True)
        dma(glubT[:], glub_d.rearrange("l (j p) -> p l j", p=128), [], ["glubT"], slow=True)
        dma(sublnT[:], subln_d.rearrange("l p -> p l"), [], ["sublnT"], slow=True)
        for l in range(L):
            for kvg in range(2):
                for a in range(2):
                    dma(sinkc[a * 64:(a + 1) * 64, l, kvg:kvg + 1],
                        sinks_d[l, 2 * kvg + a:2 * kvg + a + 1].partition_broadcast(64),
                        [], ["sinkc"], slow=True)
        act(sinkc[:], sinkc[:], AF.Exp, ["sinkc"], ["sinkc"])
        act(condT[:], condT[:], AF.Silu, ["condT"], ["condT"])

    def ada_mod():
        stg = [(wstage[0][:].rearrange("p c n -> p (c n)"), ["wstage0"]),
               (wstage[1][:].rearrange("p c n -> p (c n)"), ["wstage1"]),
               ftview(0), ftview(2), ftview(4)]
        rows, rowsn = ftview(6)
        k = 0
        for l in range(L):
            for j in range(3):
                for c in range(8):
                    sv, sn = stg[k % len(stg)]
                    k += 1
                    dma(sv, adaw_d[l, c * 128:(c + 1) * 128, j * 1024:(j + 1) * 1024], [], sn)
                    for hf in range(2):
                        mm(psb[hf][0:NSEQ, :], condT[:, c, :], sv[:, hf * 512:(hf + 1) * 512],
                           sn + ["condT"], [f"ps{hf}"], start=(c == 0), stop=(c == 7))
                dma(rows[0:NSEQ, :], adab_d[l, j * 1024:(j + 1) * 1024].partition_broadcast(NSEQ), [], rowsn,
                    slow=True)
                for hf in range(2):
                    tt("dve", rows[0:NSEQ, hf * 512:(hf + 1) * 512], psb[hf][0:NSEQ, :],
                       rows[0:NSEQ, hf * 512:(hf + 1) * 512], ALU.add, [f"ps{hf}"] + rowsn, rowsn)
                if j == 2:
                    dma(modscr_d[l], rows[0:NSEQ, :], rowsn, ["modscr"])
                else:
                    for c in range(8):
                        tr(psb[2][:, c * NSEQ:(c + 1) * NSEQ], rows[0:NSEQ, c * 128:(c + 1) * 128],
                           ident_f[0:NSEQ, 0:NSEQ], rowsn + ["ident_f"], ["ps2"], gend=(c == 7))
                    cp("dve", modT[:, l, j * 8:(j + 1) * 8, :],
                       psb[2][:, 0:8 * NSEQ].rearrange("p (c b) -> p c b", c=8), ["ps2"], ["modT"])
            for j in range(8):
                ts("dve", gmodT[:, l, j, :], modT[:, l, 8 + j, :], 1.0, ngT[:, l, j:j + 1],
                   ALU.add, ALU.mult, ["modT", "ngT"], ["gmodT"])

    wstate = {"n": 0}

    def load_wtile(srcs):
        i = wstate["n"]
        wstate["n"] += 1
        st = i % NSTG
        wb = i % NWB
        for ap, c0, ncol in srcs:
            dma(wstage[st][:, :, c0:c0 + ncol], ap.rearrange("(c p) n -> p c n", p=128),
                [], [f"wstage{st}"])
        cp("pool", wbf[wb][:], wstage[st][:], [f"wstage{st}"], [f"wbf{wb}"])
        return wb

    evac_rr = {"n": 0}

    def rstd_col(src_col, dst_col, n):
        ts("dve", dst_col, src_col, 1.0 / n, EPS, ALU.mult, ALU.add, ["small"], ["small"])
        act(dst_col, dst_col, AF.Ln, ["small"], ["small"])
        act(dst_col, dst_col, AF.Exp, ["small"], ["small"], scale=-0.5)

    def norm_to_hT(l, b):
        junk, junkn = ftview(0)
        for t in range(NT):
            par = t % 2
            xsv, xsn = ftview(2 + 2 * par)
            sm = f"small{par}"
            col = small[:, 2 * par:2 * par + 1]
            col2 = small[:, 2 * par + 1:2 * par + 2]
            act(junk, xres[:, t, :], AF.Square, [f"xres{t}"], junkn + [sm], accum=col)
            ts("dve", col2, col, 1.0 / D, EPS, ALU.mult, ALU.add, [sm], [sm])
            act(col2, col2, AF.Ln, [sm], [sm])
            act(col2, col2, AF.Exp, [sm], [sm], scale=-0.5)
            ts("dve", xsv, xres[:, t, :], col2, None, ALU.mult, None, [f"xres{t}", sm], xsn)
            for half in range(2):
                pb = 4 + 2 * par + half
                for q in range(4):
                    c = half * 4 + q
                    tr(psb[pb][:, q * 128:(q + 1) * 128], xsv[:, c * 128:(c + 1) * 128], ident_f[:],
                       xsn + ["ident_f"], [f"ps{pb}"], gend=(q == 3))
                for q in range(4):
                    c = half * 4 + q
                    dst = hT[:, c, t * 128:(t + 1) * 128]
                    src = psb[pb][:, q * 128:(q + 1) * 128]
                    if half == 0:
                        ts("dve", dst, src, gmodT[:, l, c, b:b + 1], modT[:, l, c, b:b + 1],
                           ALU.mult, ALU.add, [f"ps{pb}", "gmodT", "modT"], [f"hT{c}"])
                    else:
                        act(dst, src, AF.Identity, [f"ps{pb}", "gmodT", "modT"], [f"hT{c}"],
                            bias=modT[:, l, c, b:b + 1], scale=gmodT[:, l, c, b:b + 1])
                    evac_rr["n"] += 1

    def evac_eng():
        evac_rr["n"] += 1
        return "dve" if evac_rr["n"] % 3 != 2 else "act"

    def proj_fm(wb, dst_tile, dst_name):
        eng = evac_eng()
        for tb in range(NB):
            pb = tb % 2
            for c in range(8):
                mm(psb[pb][:, :], wbf[wb][:, c, :], hT[:, c, tb * 512:(tb + 1) * 512],
                   [f"wbf{wb}", f"hT{c}"], [f"ps{pb}"], start=(c == 0), stop=(c == 7))
            cp(eng, dst_tile[:, tb * 512:(tb + 1) * 512], psb[pb][:, :], [f"ps{pb}"], [dst_name])

    def proj_tok(wb):
        eng = evac_eng()
        for t4 in range(NT // 4):
            pb = t4 % 2
            for q in range(4):
                t = t4 * 4 + q
                for c in range(8):
                    mm(psb[pb][:, q * 128:(q + 1) * 128], hT[:, c, t * 128:(t + 1) * 128], wbf[wb][:, c, :],
                       [f"wbf{wb}", f"hT{c}"], [f"ps{pb}"], start=(c == 0), stop=(c == 7))
            cp(eng, vtok[:, t4 * 4:(t4 + 1) * 4, :], psb[pb][:, :].rearrange("p (q n) -> p q n", q=4),
               [f"ps{pb}"], ["vtok"])

    def wcols(l, c0, n=128):
        return win_d[l, :, c0:c0 + n]

    def S(i):
        return sp_s[:, i, :]

    def bc16(ap):
        return ap.unsqueeze(2).to_broadcast([128, 16, 16])

    def v3(ap):
        return ap.rearrange("p (g h) -> p g h", g=16)

    def ssm_prep(l):
        R, W = ["sp_s"], ["sp_s"]
        zl = ft[0]
        dma(zl[0:16, 0:64], lamre_d[l], [], ["ft0"])
        dma(zl[0:16, 64:128], lamre_d[l], [], ["ft0"])
        dma(zl[0:16, 128:192], lamim_d[l], [], ["ft0"])
        dma(zl[0:16, 192:256], lamim_d[l], [], ["ft0"])
        tr(psb[0][:, 0:16], zl[0:16, 0:128], ident_f[0:16, 0:16], ["ft0", "ident_f"], ["ps0"])
        tr(psb[0][:, 16:32], zl[0:16, 128:256], ident_f[0:16, 0:16], ["ft0", "ident_f"], ["ps0"])
        cp("dve", S(0), psb[0][:, 0:16], ["ps0"], W)
        cp("dve", S(1), psb[0][:, 16:32], ["ps0"], W)
        dma(S(2), lstep_d[l].partition_broadcast(128), [], W, slow=True)
        act(S(3), S(2), AF.Exp, R, W)
        tt("dve", S(4), S(0), S(3), ALU.mult, R, W)
        tt("dve", S(5), S(1), S(3), ALU.mult, R, W)
        act(rho[:], S(4), AF.Exp, R, ["rho"])
        ts("dve", S(6), S(5), 1.0 / TWO_PI, None, ALU.mult, None, R, W)
        cp("dve", itile[:, 0, 0:16], S(6), R, ["itile"])
        tt("dve", frq[:], S(6), itile[:, 0, 0:16], ALU.subtract, R + ["itile"], ["frq"])
        ts("dve", itile[:, 0, 16:32], S(6), 0.25, None, ALU.add, None, R, ["itile"])
        tt("dve", S(7), S(6), itile[:, 0, 16:32], ALU.subtract, R + ["itile"], W)
        act(S(8), frq[:], AF.Sin, ["frq"], W, scale=SC2PI)
        act(S(9), S(7), AF.Sin, R, W, scale=SC2PI, bias=HALFPI)
        tt("dve", S(10), rho[:], S(9), ALU.mult, R + ["rho"], W)
        tt("dve", S(11), rho[:], S(8), ALU.mult, R + ["rho"], W)
        ts("dve", S(12), S(10), -1.0, None, ALU.add, None, R, W)
        tt("dve", S(13), S(0), S(0), ALU.mult, R, W)
        tt("dve", S(14), S(1), S(1), ALU.mult, R, W)
        tt("dve", S(13), S(13), S(14), ALU.add, R, W)
        op("dve", lambda e: e.reciprocal(S(13), S(13)), R, W)
        tt("dve", S(14), S(12), S(0), ALU.mult, R, W)
        tt("dve", S(15), S(11), S(1), ALU.mult, R, W)
        tt("dve", S(14), S(14), S(15), ALU.add, R, W)
        tt("dve", S(14), S(14), S(13), ALU.mult, R, W)
        tt("dve", S(15), S(11), S(0), ALU.mult, R, W)
        tt("dve", S(16), S(12), S(1), ALU.mult, R, W)
        tt("dve", S(15), S(15), S(16), ALU.subtract, R, W)
        tt("dve", S(15), S(15), S(13), ALU.mult, R, W)
        ts("dve", S(16), S(15), sgn[:, 0:1], None, ALU.mult, None, R + ["sgn"], W)
        ts("dve", S(17), S(14), sgn[:, 0:1], -1.0, ALU.mult, ALU.mult, R + ["sgn"], W)
        ts("dve", S(7), frq[:], 512.0, None, ALU.mult, None, ["frq"], W)
        cp("dve", itile[:, 0, 0:16], S(7), R, ["itile"])
        tt("dve", S(19), S(7), itile[:, 0, 0:16], ALU.subtract, R + ["itile"], W)
        ts("dve", itile[:, 0, 16:32], S(7), 0.25, None, ALU.add, None, R, ["itile"])
        tt("dve", S(18), S(7), itile[:, 0, 16:32], ALU.subtract, R + ["itile"], W)
        act(S(19), S(19), AF.Sin, R, W, scale=SC2PI)
        act(S(18), S(18), AF.Sin, R, W, scale=SC2PI, bias=HALFPI)
        ts("dve", S(19), S(19), sgn[:, 0:1], None, ALU.mult, None, R + ["sgn"], W)
        X1 = v3(ft[1][:, 0:256])
        X2 = v3(ft[2][:, 0:256])
        T = v3(ft[3][:, 0:256])
        dma(X1[0:64], bre_d[l].rearrange("g p h -> p g h"), [], ["ft1"], slow=True)
        dma(X1[64:128], bim_d[l].rearrange("g p h -> p g h"), [], ["ft1"], slow=True)
        dma(X2[0:64], bim_d[l].rearrange("g p h -> p g h"), [], ["ft2"], slow=True)
        dma(X2[64:128], bre_d[l].rearrange("g p h -> p g h"), [], ["ft2"], slow=True)
        tt("dve", T, X1, bc16(S(14)), ALU.mult, ["ft1", "sp_s"], ["ft3"])
        tt("dve", sp_a[:], X2, bc16(S(16)), ALU.mult, ["ft2", "sp_s"], ["sp_a"])
        tt("dve", sp_a[:], sp_a[:], T, ALU.add, ["sp_a", "ft3"], ["sp_a"])
        tt("dve", T, X2, bc16(S(17)), ALU.mult, ["ft2", "sp_s"], ["ft3"])
        tt("dve", sp_b[:], X1, bc16(S(15)), ALU.mult, ["ft1", "sp_s"], ["sp_b"])
        tt("dve", sp_b[:], sp_b[:], T, ALU.add, ["sp_b", "ft3"], ["sp_b"])
        for mc in range(2):
            Z1 = ft[4]
            Z2 = ft[5]
            cr = cre_d[l, mc * 8:(mc + 1) * 8].rearrange("g h p -> (g h) p")
            ci = cim_d[l, mc * 8:(mc + 1) * 8].rearrange("g h p -> (g h) p")
            dma(Z1[:, 0:64], cr, [], ["ft4"])
            dma(Z1[:, 64:128], ci, [], ["ft4"])
            dma(Z2[:, 0:64], ci, [], ["ft5"])
            dma(Z2[:, 64:128], cr, [], ["ft5"])
            tr(psb[1][:, 0:128], Z1[:, 0:128], ident_f[:], ["ft4", "ident_f"], ["ps1"])
            tr(psb[1][:, 128:256], Z2[:, 0:128], ident_f[:], ["ft5", "ident_f"], ["ps1"])
            cp("dve", sp_c[:, mc * 8:(mc + 1) * 8, :], psb[1][:, 0:128].rearrange("p (g h) -> p g h", g=8),
               ["ps1"], ["sp_c"])
            cp("dve", sp_d[:, mc * 8:(mc + 1) * 8, :], psb[1][:, 128:256].rearrange("p (g h) -> p g h", g=8),
               ["ps1"], ["sp_d"])
        gst = ft[3].rearrange("p (k n) -> p k n", k=2)
        dma(gst, gluw_d[l].rearrange("(k p) n -> p k n", p=128), [], ["ft3"])
        cp("pool", gluw_sb[:], gst, ["ft3"], ["gluw_sb"])

    def ssm_weights(mc):
        s1 = sp_a[:, mc * 8:(mc + 1) * 8, :].rearrange("p g h -> p (g h)")
        s2 = sp_b[:, mc * 8:(mc + 1) * 8, :].rearrange("p g h -> p (g h)")
        tr(psb[0][:, 0:128], s1, ident_f[:], ["sp_a", "ident_f"], ["ps0"])
        tr(psb[1][:, 128:256], s2, ident_f[:], ["sp_b", "ident_f"], ["ps1"])
        for gi in range(8):
            ts("dve", ssmw[:, gi, 0, :], psb[0][:, 0:128], rowmask[:, gi:gi + 1], None, ALU.mult, None,
               ["ps0", "rowmask"], [f"ssmw0_{gi}"])
            act(ssmw[:, gi, 1, :], psb[1][:, 128:256], AF.Identity, ["ps1", "rowmask"], [f"ssmw1_{gi}"],
                scale=rowmask[:, gi:gi + 1])
        d1 = bass.AP(ssmw, 2 * 128, [[8 * 4 * 128, 128], [4 * 128 + 16, 8], [1, 16]])
        d2 = bass.AP(ssmw, 3 * 128, [[8 * 4 * 128, 128], [4 * 128 + 16, 8], [1, 16]])
        ts("dve", d1, sp_c[:, mc * 8:(mc + 1) * 8, :], sgn[:, 0:1], -1.0, ALU.mult, ALU.mult,
           ["sp_c", "sgn"], ["ssmw2"])
        ts("dve", d2, sp_d[:, mc * 8:(mc + 1) * 8, :], -1.0, None, ALU.mult, None, ["sp_d"], ["ssmw3"])

    def ssm_phase(l, b):
        ssm_prep(l)
        for mc in range(2):
            wb = load_wtile([(wcols(l, mc * 128), 0, 128)])
            proj_fm(wb, fm[mc], f"fm{mc}")
            ssm_weights(mc)
            def gen_tables(gi):
                g = mc * 8 + gi
                k = gi % 2
                A, Bt = ft[k * 2], ft[k * 2 + 1]
                An, Bn = f"ft{k * 2}", f"ft{k * 2 + 1}"
                I0 = itile[:, 0, :]
                In = "itile"
                op("pool", lambda e, A=A: e.iota(A, [[1, 512]], base=0, channel_multiplier=0,
                                                 allow_small_or_imprecise_dtypes=True), [], [An])
                pts(A, A, frq[:, g:g + 1], [An, "frq"], [An])
                cp("pool", I0, A, [An], [In])
                tt("pool", Bt, A, I0, ALU.subtract, [An, In], [Bn])
                pts(I0, A, 0.25, [An], [In], s2=1.0, op0=ALU.add, op1=ALU.mult)
                tt("pool", A, A, I0, ALU.subtract, [An, In], [An])
                act(Bt, Bt, AF.Sin, [Bn], [Bn], scale=SC2PI)
                act(A, A, AF.Sin, [An], [An], scale=SC2PI, bias=HALFPI)

            gen_tables(0)
            iters = [(gi, tb) for gi in range(8) for tb in range(NB)]

            def bufs_of(i):
                gi, tb = iters[i]
                k = gi % 2
                kk = i % 2
                return dict(gi=gi, tb=tb, g=mc * 8 + gi, kk=kk,
                            A=ft[k * 2], Bt=ft[k * 2 + 1], An=f"ft{k * 2}", Bn=f"ft{k * 2 + 1}",
                            T1=ft[4 + kk], T2=ft[6 + kk], T1n=f"ft{4 + kk}", T2n=f"ft{6 + kk}",
                            P1=bt[kk * 2][:], P2=bt[kk * 2 + 1][:], P1n=f"bt{kk * 2}", P2n=f"bt{kk * 2 + 1}",
                            cols=slice(tb * 512, (tb + 1) * 512))

            def stage_bu(i):
                d = bufs_of(i)
                pa = d["kk"]
                mm(psb[pa][:, :], ssmw[:, d["gi"], 0, :], fm[mc][:, d["cols"]],
                   [f"ssmw0_{d['gi']}", f"fm{mc}"], [f"ps{pa}"])
                mm(psb[3][:, :], ssmw[:, d["gi"], 1, :], fm[mc][:, d["cols"]],
                   [f"ssmw1_{d['gi']}", f"fm{mc}"], ["ps3"])

            def stage_dve(i):
                d = bufs_of(i)
                pa, g, tb = d["kk"], d["g"], d["tb"]
                A, Bt, T1, T2, P1, P2 = d["A"], d["Bt"], d["T1"], d["T2"], d["P1"], d["P2"]
                An, Bn, T1n, T2n, P1n, P2n = d["An"], d["Bn"], d["T1n"], d["T2n"], d["P1n"], d["P2n"]
                tt("dve", T1, psb[pa][:, :], A, ALU.mult, [f"ps{pa}", An], [T1n])
                tt("dve", T2, psb[3][:, :], Bt, ALU.mult, ["ps3", Bn], [T2n])
                tt("dve", T1, T1, T2, ALU.add, [T1n, T2n], [T1n])
                if tb == 0:
                    init = 0.0
                else:
                    mm(psb[2][:, 0:1], psw[:], carry[:, g:g + 1], ["psw", "carry"], ["ps2"])
                    act(small[:, 20:21], psb[2][:, 0:1], AF.Identity, ["ps2", "sp_s"], ["small"],
                        scale=sp_s[:, 19, g:g + 1])
                    stt(small[:, 21:22], carry[:, g:g + 1], sp_s[:, 18, g:g + 1], small[:, 20:21],
                        ALU.mult, ALU.add, ["carry", "sp_s", "small"], ["small"])
                    init = small[:, 21:22]
                op("dve", lambda e, T2=T2, T1=T1, g=g, init=init: e.tensor_tensor_scan(
                    T2, rho[:, g:g + 1].to_broadcast([128, 512]), T1, init, ALU.mult, ALU.add),
                   [T1n, "rho", "small"], [T2n])
                if tb < NB - 1:
                    cp("act", carry[:, g:g + 1], T2[:, 511:512], [T2n], ["carry"])
                tt("pool", P1, A, T2, ALU.mult, [An, T2n], [P1n])
                tt("dve", P2, Bt, T2, ALU.mult, [Bn, T2n], [P2n])

            def stage_y(i):
                d = bufs_of(i)
                gi, tb = d["gi"], d["tb"]
                mm(psb[4 + tb][:, :], ssmw[:, gi, 2, :], d["P1"], ["ssmw2", d["P1n"]], [f"ps{4 + tb}"],
                   start=(gi == 0), stop=False)
                mm(psb[4 + tb][:, :], ssmw[:, gi, 3, :], d["P2"], ["ssmw3", d["P2n"]], [f"ps{4 + tb}"],
                   start=False, stop=(gi == 7))

            for i in range(len(iters)):
                stage_bu(i)
                stage_dve(i)
                if i > 0:
                    stage_y(i - 1)
                gi, tb = iters[i]
                if tb == 0 and gi < 7:
                    gen_tables(gi + 1)
            stage_y(len(iters) - 1)
            for tb in range(NB):
                cols = slice(tb * 512, (tb + 1) * 512)
                Y, X2_ = ft[4 + tb % 2], ft[6 + tb % 2]
                Yn, Xn = f"ft{4 + tb % 2}", f"ft{6 + tb % 2}"
                stt(Y, fm[mc][:, cols], ssmdT[:, l, mc:mc + 1], psb[4 + tb][:, :], ALU.mult, ALU.add,
                    [f"fm{mc}", "ssmdT", f"ps{4 + tb}"], [Yn])
                act(X2_, Y, AF.Square, [Yn], [Xn])
                act(X2_, X2_, AF.Identity, [Xn], [Xn], bias=1.0, scale=0.044715)
                tt("dve", X2_, X2_, Y, ALU.mult, [Xn, Yn], [Xn])
                act(X2_, X2_, AF.Sigmoid, [Xn], [Xn], scale=1.5957691216)
                tt("dve", mixT[:, mc, cols], Y, X2_, ALU.mult, [Yn, Xn], ["mixT"])
        for nt in range(2):
            for tb in range(NB):
                cols = slice(tb * 512, (tb + 1) * 512)
                pb = tb % 2
                for k in range(2):
                    mm(psb[pb][:, :], gluw_sb[:, k, nt * 128:(nt + 1) * 128], mixT[:, k, cols],
                       ["gluw_sb", "mixT"], [f"ps{pb}"], start=(k == 0), stop=(k == 1))
                act(fm[1 + nt][:, cols], psb[pb][:, :], AF.Sigmoid, [f"ps{pb}", "glubT"], [f"fm{1 + nt}"],
                    bias=glubT[:, l, nt:nt + 1])
        for nt in range(2):
            wb = load_wtile([(wcols(l, 256 + nt * 128), 0, 128)])
            proj_fm(wb, fm[0], "fm0")
            act(fm[0][:, :], fm[0][:, :], AF.Silu, ["fm0"], ["fm0"])
            tt("dve", fm[1 + nt][:, :], fm[1 + nt][:, :], fm[0][:, :], ALU.mult, [f"fm{1 + nt}", "fm0"],
               [f"fm{1 + nt}"])
            tt("dve", mixT[:, nt, :], mixT[:, nt, :], fm[1 + nt][:, :], ALU.mult, ["mixT", f"fm{1 + nt}"],
               ["mixT"])

    NLAM, GCOL = small[:, 8:9], small[:, 9:10]

    def diff_prep(l):
        lam_init = 0.8 - 0.6 * math.exp(-0.3 * l)
        memset("pool", fm[1][64:128, :], 0.0, ["fm1"])
        memset("pool", iraw[0:64, :], 0.0, ["iraw"])
        lamb = ft[5][:, 0:256].rearrange("p (i n) -> p i n", i=4)
        for i, d_ in enumerate((lq1_d, lk1_d, lq2_d, lk2_d)):
            dma(lamb[:, i, :], d_[l].partition_broadcast(128), [], ["ft5"], slow=True)
        tt("dve", lamb[:, 0, :], lamb[:, 0, :], lamb[:, 1, :], ALU.mult, ["ft5"], ["ft5"])
        tt("dve", lamb[:, 2, :], lamb[:, 2, :], lamb[:, 3, :], ALU.mult, ["ft5"], ["ft5"])
        op("dve", lambda e: e.reduce_sum(small[:, 10:11], lamb[:, 0, :], AX.X), ["ft5"], ["small"])
        op("dve", lambda e: e.reduce_sum(small[:, 11:12], lamb[:, 2, :], AX.X), ["ft5"], ["small"])
        act(small[:, 10:12], small[:, 10:12], AF.Exp, ["small"], ["small"])
        tt("dve", NLAM, small[:, 11:12], small[:, 10:11], ALU.subtract, ["small"], ["small"])
        ts("dve", NLAM, NLAM, -lam_init, None, ALU.add, None, ["small"], ["small"])
        ts("dve", GCOL, sublnT[:, l:l + 1], 1.0 - lam_init, None, ALU.mult, None, ["sublnT"], ["small"])

    def diff_head(l, h):
        wq = load_wtile([(wcols(l, 512 + 128 * h), 0, 128)])
        proj_fm(wq, fm[0], "fm0")
        wk = load_wtile([(wcols(l, 1024 + 128 * h), 0, 128)])
        for tb in range(NB):
            pb = tb % 2
            for c in range(8):
                mm(psb[pb][:, :], wbf[wk][:, c, :], hT[:, c, tb * 512:(tb + 1) * 512],
                   [f"wbf{wk}", f"hT{c}"], [f"ps{pb}"], start=(c == 0), stop=(c == 7))
            cs_ = slice(tb * 512, (tb + 1) * 512)
            cp("dve", fm[1][0:64, cs_], psb[pb][0:64, :], [f"ps{pb}"], ["fm1"])
            cp("dve", iraw[64:128, cs_], psb[pb][64:128, :], [f"ps{pb}"], ["iraw"])
        wv = load_wtile([(wcols(l, 1536 + 128 * h), 0, 128)])
        proj_tok(wv)
        wz = load_wtile([(wcols(l, 2048 + 128 * h), 0, 128)])
        proj_fm(wz, fm[2], "fm2")
        act(fm[2][:, :], fm[2][:, :], AF.Silu, ["fm2"], ["fm2"])
        tiles = []
        for qc in range(NB):
            nkb = 4 * qc + 4
            for a in range(2):
                for kb in range(nkb):
                    tiles.append((qc, a, kb, nkb))
        LA = 2
        SB = (1, 2, 3)

        def emit_qk(i):
            qc, a, kb, nkb = tiles[i]
            rows = slice(a * 64, (a + 1) * 64)
            j = kb - 4 * qc
            c0 = 128 * j if j >= 0 else 0
            sb_ = SB[i % 3]
            kt = fm[1] if a == 0 else iraw
            mm(psb[sb_][:, c0:512], kt[:, kb * 128:(kb + 1) * 128],
               fm[0][:, qc * 512 + c0:(qc + 1) * 512], ["fm0", "fm1", "itile", "itile0", "itile1"], [f"ps{sb_}"],
               start=True, stop=(j < 0))
            if j >= 0:
                mm(psb[sb_][:, c0:c0 + 128], ident_b[:], mask_d[:], ["ident_b", "mask_d"], [f"ps{sb_}"],
                   start=False, stop=True)

        def emit_rest(i):
            qc, a, kb, nkb = tiles[i]
            j = kb - 4 * qc
            c0 = 128 * j if j >= 0 else 0
            sb_ = SB[i % 3]
            accb, sumb = 4 + a, 6 + a
            PT = bt[i % NBT]
            PTn = f"bt{i % NBT}"
            di = 4 * qc - kb + 3
            act(PT[:, c0:512], psb[sb_][:, c0:512], AF.Exp, [f"ps{sb_}", "biasd"], [PTn],
                bias=biasd[:, h, di:di + 1], scale=0.125)
            mm(psb[accb][:, c0:512], vtok[:, kb, :], PT[:, c0:512], ["vtok", PTn], [f"ps{accb}"],
               start=(kb == 0), stop=(kb == nkb - 1))
            mm(psb[sumb][:, c0:512], ones_b[:], PT[:, c0:512], ["ones_b", PTn], [f"ps{sumb}"],
               start=(kb == 0), stop=(kb == nkb - 1))

        def epilogue(qc):
            cols = slice(qc * 512, (qc + 1) * 512)
            r1, r2, o1, o2, sq = ft[0], ft[1], ft[2], ft[3], ft[4]
            act(r1, psb[6][:, :], AF.Ln, ["ps6"], ["ft0"])
            act(r2, psb[7][:, :], AF.Ln, ["ps7"], ["ft1"])
            cp("dve", o1, psb[4][:, :], ["ps4"], ["ft2"])
            cp("dve", o2, psb[5][:, :], ["ps5"], ["ft3"])
            steps = [
                lambda: act(r1, r1, AF.Exp, ["ft0"], ["ft0"], scale=-1.0),
                lambda: act(r2, r2, AF.Exp, ["ft1"], ["ft1"], scale=-1.0),
                lambda: tt("dve", o1, o1, r1, ALU.mult, ["ft2", "ft0"], ["ft2"]),
                lambda: tt("dve", o2, o2, r2, ALU.mult, ["ft3", "ft1"], ["ft3"]),
                lambda: stt(o1, o2, NLAM, o1, ALU.mult, ALU.add, ["ft3", "small", "ft2"], ["ft2"]),
                lambda: act(sq, o1, AF.Square, ["ft2"], ["ft4"]),
                lambda: None,
                lambda: mm(psb[0][:, :], ones_f[:], sq, ["ones_f", "ft4"], ["ps0"]),
                lambda: None,
                lambda: ts("dve", r1, psb[0][:, :], 1.0 / 128.0, EPS, ALU.mult, ALU.add, ["ps0"], ["ft0"]),
                lambda: act(r1, r1, AF.Ln, ["ft0"], ["ft0"]),
                lambda: act(r1, r1, AF.Exp, ["ft0"], ["ft0"], scale=-0.5),
                lambda: tt("dve", o1, o1, r1, ALU.mult, ["ft2", "ft0"], ["ft2"]),
                lambda: stt(mixT[:, 2 + h, cols], o1, GCOL, fm[2][:, cols], ALU.mult, ALU.mult,
                            ["ft2", "small", "fm2"], ["mixT"]),
            ]
            return steps

        n = len(tiles)
        pending = []
        for i in range(min(LA, n)):
            emit_qk(i)
        for i in range(n):
            emit_rest(i)
            if i + LA < n:
                emit_qk(i + LA)
            if pending:
                pending.pop(0)()
            qc, a, kb, nkb = tiles[i]
            if a == 1 and kb == nkb - 1:
                while pending:
                    pending.pop(0)()
                pending = epilogue(qc)
        while pending:
            pending.pop(0)()

    def swa_phase(l, b):
        wv = load_wtile([(wcols(l, 2944), 0, 128)])
        proj_tok(wv)
        pt_i = 0
        for kvg in range(2):
            wq = load_wtile([(wcols(l, 2560 + 128 * kvg), 0, 128)])
            proj_fm(wq, fm[0], "fm0")
            wk = load_wtile([(wcols(l, 2816 + 64 * kvg, 64), 0, 64), (wcols(l, 2816 + 64 * kvg, 64), 64, 64)])
            proj_fm(wk, fm[1], "fm1")
            wz = load_wtile([(wcols(l, 3072 + 128 * kvg), 0, 128)])
            proj_fm(wz, fm[2], "fm2")
            if kvg == 1 and "out" in PHASES:
                out_proj_load(l, b)
            act(fm[2][:, :], fm[2][:, :], AF.Silu, ["fm2"], ["fm2"])
            vc = slice(kvg * 64, (kvg + 1) * 64)

            def emit_qk(n):
                qcols = slice(n * 128, (n + 1) * 128)
                for a in range(2):
                    hh = 2 * kvg + a
                    rows = slice(a * 64, (a + 1) * 64)
                    sb_ = 2 * (n % 2) + a
                    sn = f"ps{sb_}"
                    if n > 0:
                        mm(psb[sb_][:, 0:128], fm[1][rows, (n - 1) * 128:n * 128], fm[0][rows, qcols],
                           ["fm0", "fm1"], [sn], start=True, stop=False)
                        mm(psb[sb_][:, 0:128], ident_b[:], swab[:, hh, 0:128], ["ident_b", "swab"],
                           [sn], start=False, stop=True)
                    mm(psb[sb_][:, 128:256], fm[1][rows, qcols], fm[0][rows, qcols],
                       ["fm0", "fm1"], [sn], start=True, stop=False)
                    mm(psb[sb_][:, 128:256], ident_b[:], swab[:, hh, 128:256], ["ident_b", "swab"],
                       [sn], start=False, stop=True)

            def emit_rest(n, PT, PTn):
                q = n % 4
                par = (n // 4) % 2
                accb, sumb = 4 + 2 * par, 5 + 2 * par
                oc = slice(q * 128, (q + 1) * 128)
                for a in range(2):
                    rows = slice(a * 64, (a + 1) * 64)
                    sb_ = 2 * (n % 2) + a
                    sn = f"ps{sb_}"
                    lo = 0 if n > 0 else 128
                    act(PT[:, a * 256 + lo:a * 256 + 256], psb[sb_][:, lo:256], AF.Exp,
                        [sn], [PTn], scale=0.125)
                    if n > 0:
                        mm(psb[accb][rows, oc], vtok[:, n - 1, vc], PT[:, a * 256:a * 256 + 128],
                           ["vtok", PTn], [f"ps{accb}"], start=True, stop=False)
                        mm(psb[sumb][rows, oc], ones_b[:, 0:64], PT[:, a * 256:a * 256 + 128],
                           ["ones_b", PTn], [f"ps{sumb}"], start=True, stop=False)
                    mm(psb[accb][rows, oc], vtok[:, n, vc], PT[:, a * 256 + 128:a * 256 + 256],
                       ["vtok", PTn], [f"ps{accb}"], start=(n == 0), stop=True)
                    mm(psb[sumb][rows, oc], ones_b[:, 0:64], PT[:, a * 256 + 128:a * 256 + 256],
                       ["ones_b", PTn], [f"ps{sumb}"], start=(n == 0), stop=True)

            def epilogue(n4):
                par = n4 % 2
                accb, sumb = 4 + 2 * par, 5 + 2 * par
                cols = slice(n4 * 512, (n4 + 1) * 512)
                den, o_ = ft[2 * par], ft[2 * par + 1]
                dn, on_ = f"ft{2 * par}", f"ft{2 * par + 1}"
                act(den, psb[sumb][:, :], AF.Ln, [f"ps{sumb}", "sinkc"], [dn], bias=sinkc[:, l, kvg:kvg + 1])
                cp("dve", o_, psb[accb][:, :], [f"ps{accb}"], [on_])
                return [
                    lambda: act(den, den, AF.Exp, [dn], [dn], scale=-1.0),
                    lambda: tt("dve", o_, o_, den, ALU.mult, [on_, dn], [on_]),
                    lambda: tt("dve", mixT[:, 6 + kvg, cols], o_, fm[2][:, cols], ALU.mult, [on_, "fm2"],
                               ["mixT"]),
                ]

            pending = []
            emit_qk(0)
            for n in range(NT):
                if n + 1 < NT:
                    emit_qk(n + 1)
                PT = bt[pt_i % NBT]
                PTn = f"bt{pt_i % NBT}"
                pt_i += 1
                emit_rest(n, PT, PTn)
                if pending:
                    pending.pop(0)()
                if n % 4 == 3:
                    while pending:
                        pending.pop(0)()
                    pending = epilogue(n // 4)
            while pending:
                pending.pop(0)()

    def wout_store():
        if wout_bf is None:
            return [hT[:, k, 0:D] for k in range(8)], [f"hT{k}" for k in range(8)]
        return [wout_bf[:, k, :] for k in range(8)], [f"wout_bf{k}" for k in range(8)]

    def out_proj_load(l, b):
        dma(bc[:], modscr_d[l, b:b + 1, :].partition_broadcast(128).rearrange("p o n -> p (o n)"),
            ["modscr"], ["bc"])
        wst, wname = wout_store()
        for k in range(8):
            i = wstate["n"]
            wstate["n"] += 1
            st = i % NSTG
            stv = wstage[st][:].rearrange("p c n -> p (c n)")
            dma(stv, wout_d[l, k * 128:(k + 1) * 128, :], [], [f"wstage{st}"])
            tt("pool", wst[k], stv, bc[:], ALU.mult, [f"wstage{st}", "bc"], [wname[k]])

    def out_proj(l, b):
        wst, wname = wout_store()
        for t in range(NT):
            for half in range(2):
                pb = (t * 2 + half) % 2
                cs = slice(half * 512, (half + 1) * 512)
                for k in range(8):
                    mm(psb[pb][:, :], mixT[:, k, t * 128:(t + 1) * 128], wst[k][:, cs], ["mixT", wname[k]],
                       [f"ps{pb}"], start=(k == 0), stop=(k == 7))
                tt("dve", xres[:, t, cs], psb[pb][:, :], xres[:, t, cs], ALU.add, [f"ps{pb}", f"xres{t}"], [f"xres{t}"])

    def final_norm(b):
        dma(bc[:], fg_d.partition_broadcast(128), [], ["bc"])
        junk, junkn = ftview(0)
        for t in range(NT):
            par = t % 3
            xsv, xsn = ftview(2 + 2 * par)
            sm = f"small{par}"
            col = small[:, 2 * par:2 * par + 1]
            col2 = small[:, 2 * par + 1:2 * par + 2]
            act(junk, xres[:, t, :], AF.Square, [f"xres{t}"], junkn + [sm], accum=col)
            ts("dve", col2, col, 1.0 / D, EPS, ALU.mult, ALU.add, [sm], [sm])
            act(col2, col2, AF.Ln, [sm], [sm])
            act(col2, col2, AF.Exp, [sm], [sm], scale=-0.5)
            stt(xsv, xres[:, t, :], col2, bc[:], ALU.mult, ALU.mult, [f"xres{t}", sm, "bc"], xsn)
            dma(out_d[b, t * 128:(t + 1) * 128, :], xsv, xsn, ["outd"])

    consts()
    small_loads()
    for t in range(NT):
        dma(xres[:, t, :], x_d[0, t * 128:(t + 1) * 128, :], [], [f"xres{t}"])
    ada_mod()
    phases = PHASES

    for b in range(NSEQ):
        if b > 0:
            for t in range(NT):
                dma(xres[:, t, :], x_d[b, t * 128:(t + 1) * 128, :], [], [f"xres{t}"])
        for l in range(L):
            norm_to_hT(l, b)
            if "ssm" in phases:
                ssm_phase(l, b)
            if "diff" in phases:
                diff_prep(l)
                for h in range(4):
                    diff_head(l, h)
            if "swa" in phases:
                swa_phase(l, b)
            if dbg is not None and l == 0 and b == 0:
                for c in range(8):
                    cp("dve", ft[2], mixT[:, c, 0:512], ["mixT"], ["ft2"])
                    dma(dbg_d[c], ft[2], ["ft2"], ["dbgout"])
            if "out" in phases:
                if "swa" not in phases:
                    out_proj_load(l, b)
                out_proj(l, b)
        final_norm(b)

    sems = {}
    for e in ("pe", "act", "dve", "pool"):
        sems[e] = es.enter_context(nc.semaphore(f"s_{e}"))
    dsems = [es.enter_context(nc.semaphore(f"s_dma{i}")) for i in range(Prog.NDMA)]
    P.resolve(sems, dsems)
    with nc.Block() as block:
        @block.sync
        def _(e):
            P.emit("sp", e, final=True)

        @block.tensor
        def _(e):
            P.emit("pe", e)

        @block.scalar
        def _(e):
            P.emit("act", e)

        @block.vector
        def _(e):
            P.emit("dve", e)

        @block.gpsimd
        def _(e):
            P.emit("pool", e)
    es.close()
    return nc


INPUT_NAMES = ["x", "c", "norm_gain", "ada_w", "ada_b", "w_in", "w_out", "ssm_lam_re", "ssm_lam_im",
               "ssm_log_step", "ssm_b_re", "ssm_b_im", "ssm_c_re", "ssm_c_im", "ssm_d", "glu_w", "glu_b",
               "diff_lq1", "diff_lk1", "diff_lq2", "diff_lk2", "diff_subln", "swa_sinks", "final_gain"]


def run(inputs, ncores, dbg=None):
    x = np.ascontiguousarray(np.asarray(inputs["x"], dtype=np.float32))
    Bt, S, _ = x.shape
    nseq = Bt // ncores
    nc = build_nc(nseq, S, dbg=dbg)
    in_maps = []
    for i in range(ncores):
        m = {}
        for k in INPUT_NAMES:
            a = np.ascontiguousarray(np.asarray(inputs[k], dtype=np.float32))
            if k in ("x", "c"):
                a = np.ascontiguousarray(a[i * nseq:(i + 1) * nseq])
            m[k] = a
        in_maps.append(m)
    res = run_bass_kernel_spmd(nc, in_maps, core_ids=list(range(ncores)))
    out = np.concatenate([np.asarray(r["out"]) for r in res.results], axis=0)
    if dbg is not None:
        return out, [np.asarray(r["dbg"]) for r in res.results]
    return out


def kernel(**inputs):
    return run(inputs, NCORES).astype(np.float32)
```

```python
import math
from contextlib import ExitStack

import numpy as np
import concourse.bass as bass
import concourse.mybir as mybir
from concourse.bass_utils import run_bass_kernel_spmd

F32 = mybir.dt.float32
BF16 = mybir.dt.bfloat16
I32 = mybir.dt.int32
ALU = mybir.AluOpType
AF = mybir.ActivationFunctionType
AX = mybir.AxisListType

D = 1024
L = 2
NCORES = 8
PHASES = ("ssm", "diff", "swa", "out")
SBUF_LEFT = []
CHECK_PSUM = False
PSUM_WARN = []
IN_COLS = 3328
EPS = 1e-6
TWO_PI = 2.0 * math.pi
SLOPES = [2.0 ** (-(i + 1)) for i in range(8)]


class Buf:
    __slots__ = ("name", "w", "r")

    def __init__(self, name):
        self.name = name
        self.w = None
        self.r = []


class Op:
    __slots__ = ("eng", "fn", "idx", "deps", "signal", "gend", "sem", "val", "waits")

    def __init__(self, eng, fn, idx, gend):
        self.eng = eng
        self.fn = fn
        self.idx = idx
        self.deps = []
        self.signal = False
        self.gend = gend
        self.sem = None
        self.val = 0
        self.waits = []


class Prog:
    ENGS = ("pe", "act", "dve", "pool", "sp")
    NDMA = 16

    def __init__(self):
        self.ops = {e: [] for e in self.ENGS}

    def add(self, eng, fn, reads=(), writes=(), gend=True):
        lst = self.ops[eng]
        op = Op(eng, fn, len(lst), gend)
        deps = []
        for b in reads:
            if CHECK_PSUM and b.name.startswith("ps") and b.name[2:3].isdigit():
                for r in b.r:
                    if r.eng != eng:
                        PSUM_WARN.append((b.name, r.eng, eng))
            if b.w is not None:
                deps.append(b.w)
        for b in writes:
            if b.w is not None:
                deps.append(b.w)
            deps.extend(b.r)
        seen = set()
        for d in deps:
            if d.eng == "pe":
                if eng == "pe":
                    continue
                pl = self.ops["pe"]
                j = d.idx
                while j < len(pl) and not pl[j].gend:
                    j += 1
                if j < len(pl):
                    d = pl[j]
            if id(d) in seen:
                continue
            seen.add(id(d))
            d.signal = True
            op.deps.append(d)
        lst.append(op)
        for b in writes:
            b.w = op
            b.r = []
        for b in reads:
            b.r.append(op)
        return op

    def resolve(self, sems, dsems):
        for e in self.ENGS:
            if e == "sp":
                cnt = [0] * self.NDMA
                for i, op in enumerate(self.ops[e]):
                    k = i % self.NDMA
                    cnt[k] += 16
                    op.signal = True
                    op.sem = dsems[k]
                    op.val = cnt[k]
                self.dma_final = [(dsems[k], cnt[k]) for k in range(self.NDMA) if cnt[k]]
            else:
                n = 0
                for op in self.ops[e]:
                    if op.signal:
                        n += 1
                        op.sem = sems[e]
                        op.val = n
        for e in self.ENGS:
            waited = {}
            for op in self.ops[e]:
                need = {}
                for d in op.deps:
                    k = id(d.sem)
                    if d.val > need.get(k, (None, 0))[1]:
                        need[k] = (d.sem, d.val)
                for k, (s, v) in need.items():
                    if waited.get(k, 0) >= v:
                        continue
                    waited[k] = v
                    op.waits.append((s, v))

    def emit(self, eng_name, e, final=False):
        for op in self.ops[eng_name]:
            if eng_name == "sp" and op.val > 16:
                e.wait_ge(op.sem, op.val - 16)
            for s, v in op.waits:
                e.wait_ge(s, v)
            ins = op.fn(e)
            if op.signal:
                ins.then_inc(op.sem, 16 if eng_name == "sp" else 1)
        if final:
            for s, v in self.dma_final:
                e.wait_ge(s, v)


def build_nc(NSEQ, S, dbg=None):
    NT = S // 128
    NB = S // 512
    nc = bass.Bass("TRN2", target_bir_lowering=False)
    P = Prog()

    def dram_in(name, shape):
        return nc.dram_tensor(name, list(shape), F32, kind="ExternalInput").ap()

    x_d = dram_in("x", [NSEQ, S, D])
    c_d = dram_in("c", [NSEQ, D])
    ng_d = dram_in("norm_gain", [L, D])
    adaw_d = dram_in("ada_w", [L, D, 3 * D])
    adab_d = dram_in("ada_b", [L, 3 * D])
    win_d = dram_in("w_in", [L, D, IN_COLS])
    wout_d = dram_in("w_out", [L, D, D])
    lamre_d = dram_in("ssm_lam_re", [L, 16, 64])
    lamim_d = dram_in("ssm_lam_im", [L, 16, 64])
    lstep_d = dram_in("ssm_log_step", [L, 16])
    bre_d = dram_in("ssm_b_re", [L, 16, 64, 16])
    bim_d = dram_in("ssm_b_im", [L, 16, 64, 16])
    cre_d = dram_in("ssm_c_re", [L, 16, 16, 64])
    cim_d = dram_in("ssm_c_im", [L, 16, 16, 64])
    ssmd_d = dram_in("ssm_d", [L, 256])
    gluw_d = dram_in("glu_w", [L, 256, 256])
    glub_d = dram_in("glu_b", [L, 256])
    lq1_d = dram_in("diff_lq1", [L, 64])
    lk1_d = dram_in("diff_lk1", [L, 64])
    lq2_d = dram_in("diff_lq2", [L, 64])
    lk2_d = dram_in("diff_lk2", [L, 64])
    subln_d = dram_in("diff_subln", [L, 128])
    sinks_d = dram_in("swa_sinks", [L, 4])
    fg_d = dram_in("final_gain", [D])
    out_d = nc.dram_tensor("out", [NSEQ, S, D], F32, kind="ExternalOutput").ap()
    modscr_d = nc.dram_tensor("modscr", [L, NSEQ, D], F32, kind="Internal").ap()
    dbg_d = None
    if dbg is not None:
        dbg_d = nc.dram_tensor("dbg", list(dbg), F32, kind="ExternalOutput").ap()

    es = ExitStack()
    bufs = {}

    def sb(name, shape, dt=F32):
        t = es.enter_context(nc.sbuf_tensor(name, list(shape), dt))
        bufs[name] = Buf(name)
        return t

    def B(name):
        if name not in bufs:
            bufs[name] = Buf(name)
        return bufs[name]

    xres = sb("xres", [128, NT, D])
    hT = sb("hT", [128, 8, S], BF16)
    mixT = sb("mixT", [128, 8, S], BF16)
    NFM = 3
    fm = [sb(f"fm{i}", [128, S], BF16) for i in range(NFM)]
    vtok = sb("vtok", [128, NT, 128], BF16)
    NSTG = 2
    wstage = [sb(f"wstage{i}", [128, 8, 128]) for i in range(NSTG)]
    NWB = 2
    wbf = [sb(f"wbf{i}", [128, 8, 128], BF16) for i in range(NWB)]
    if S >= 1024:
        wout_bf = None
    else:
        wout_bf = sb("wout_bf", [128, 8, D], BF16)
    NF = 8
    ftb = sb("ftb", [128, NF, 512])
    ft = [ftb[:, i, :] for i in range(NF)]
    xs_ap = ftb[:, 6:8, :].rearrange("p a n -> p (a n)")
    iraw = sb("iraw", [128, S], BF16)
    itile = sb("itile", [128, 1, 512], I32)
    NBT = 4
    bt = [sb(f"bt{i}", [128, 512], BF16) for i in range(NBT)]
    bc = sb("bc", [128, D])
    ssmw = sb("ssmw", [128, 8, 4, 128], BF16)
    gluw_sb = sb("gluw_sb", [128, 2, 256], BF16)
    ident_f = sb("ident_f", [128, 128])
    ones_f = sb("ones_f", [128, 128])
    ident_b = sb("ident_b", [128, 128], BF16)
    ones_b = sb("ones_b", [128, 128], BF16)
    rowmask = sb("rowmask", [128, 8])
    psw = sb("psw", [128, 128])
    sgn = sb("sgn", [128, 1])
    pidx = sb("pidx", [128, 19])
    biasd = sb("biasd", [128, 4, 19])
    mask_d = sb("mask_d", [128, 128], BF16)
    swab = sb("swab", [128, 4, 256], BF16)
    condT = sb("condT", [128, 8, NSEQ])
    modT = sb("modT", [128, L, 24, NSEQ])
    adabT = sb("adabT", [128, L, 24])
    ngT = sb("ngT", [128, L, 8])
    gmodT = sb("gmodT", [128, L, 8, NSEQ])
    small = sb("small", [128, 64])
    sp_a = sb("sp_a", [128, 16, 16])
    sp_b = sb("sp_b", [128, 16, 16])
    sp_c = sb("sp_c", [128, 16, 16])
    sp_d = sb("sp_d", [128, 16, 16])
    sp_s = sb("sp_s", [128, 20, 16])
    ssmdT = sb("ssmdT", [128, L, 2])
    glubT = sb("glubT", [128, L, 2])
    sublnT = sb("sublnT", [128, L])
    sinkc = sb("sinkc", [128, L, 2])
    frq = sb("frq", [128, 16])
    rho = sb("rho", [128, 16])
    carry = sb("carry", [128, 16])
    psb = [es.enter_context(nc.psum_tensor(f"ps{i}", [128, 512], F32)) for i in range(8)]
    for i in range(8):
        bufs[f"ps{i}"] = Buf(f"ps{i}")

    SBUF_LEFT.append(nc.sbuf_bytes_remaining)
    def dma(out_ap, in_ap, reads, writes, slow=False):
        def fn(e):
            if slow:
                return e.dma_start(out=out_ap, in_=in_ap, allow_slow_non_contiguous=True)
            return e.dma_start(out=out_ap, in_=in_ap)
        return P.add("sp", fn, [B(r) for r in reads], [B(w) for w in writes])

    def op(eng, fn, reads, writes, gend=True):
        return P.add(eng, fn, [B(r) for r in reads], [B(w) for w in writes], gend)

    def mm(out_ap, lhsT, rhs, reads, writes, start=True, stop=True):
        return op("pe", lambda e: e.matmul(out_ap, lhsT, rhs, start=start, stop=stop),
                  reads, writes, gend=stop)

    def tr(out_ap, in_ap, ident, reads, writes, gend=True):
        return op("pe", lambda e: e.transpose(out_ap, in_ap, ident), reads, writes, gend=gend)

    def act(out_ap, in_ap, func, reads, writes, bias=None, scale=None, accum=None):
        kw = {}
        if bias is not None:
            kw["bias"] = bias
        if scale is not None:
            kw["scale"] = scale
        if accum is not None:
            kw["accum_out"] = accum
        return op("act", lambda e: e.activation(out_ap, in_ap, func, **kw), reads, writes)

    def ts(eng, out_ap, in_ap, s1, s2, op0, op1, reads, writes):
        if s2 is None:
            return op(eng, lambda e: e.tensor_scalar(out_ap, in_ap, s1, None, op0), reads, writes)
        return op(eng, lambda e: e.tensor_scalar(out_ap, in_ap, s1, s2, op0, op1), reads, writes)

    def tt(eng, out_ap, a, b, o, reads, writes):
        return op(eng, lambda e: e.tensor_tensor(out_ap, a, b, o), reads, writes)

    def stt(out_ap, in0, scalar, in1, op0, op1, reads, writes):
        return op("dve", lambda e: e.scalar_tensor_tensor(out_ap, in0, scalar, in1, op0, op1),
                  reads, writes)

    def cp(eng, out_ap, in_ap, reads, writes):
        if eng == "act":
            return op("act", lambda e: e.copy(out_ap, in_ap), reads, writes)
        return op(eng, lambda e: e.tensor_copy(out_ap, in_ap), reads, writes)

    def memset(eng, ap, val, writes):
        return op(eng, lambda e: e.memset(ap, val), [], writes)

    def ftview(i):
        return ftb[:, i:i + 2, :].rearrange("p a n -> p (a n)"), [f"ft{i}", f"ft{i + 1}"]

    SC2PI = 6.283185
    HALFPI = 1.5707963

    def pts(out_ap, in_ap, s1, reads, writes, s2=0.0, op0=ALU.mult, op1=ALU.add):
        return op("pool", lambda e: e.tensor_scalar(out_ap, in_ap, s1, s2, op0, op1), reads, writes)

    def consts():
        memset("pool", ones_f[:], 1.0, ["ones_f"])
        memset("pool", ones_b[:], 1.0, ["ones_b"])
        op("pool", lambda e: e.affine_select(ident_f[:], ones_f[:], [[-1, 128]], ALU.is_equal, 0.0,
                                             base=0, channel_multiplier=1), ["ones_f"], ["ident_f"])
        cp("pool", ident_b[:], ident_f[:], ["ident_f"], ["ident_b"])
        op("pool", lambda e: e.affine_select(rowmask[:], ones_f[:, 0:8], [[-16, 8]], ALU.is_ge, 0.0,
                                             base=0, channel_multiplier=1), ["ones_f"], ["rowmask"])
        op("pool", lambda e: e.affine_select(rowmask[:], rowmask[:], [[16, 8]], ALU.is_ge, 0.0,
                                             base=15, channel_multiplier=-1), ["rowmask"], ["rowmask"])
        op("pool", lambda e: e.affine_select(sgn[:], ones_f[:, 0:1], [[0, 1]], ALU.is_ge, -1.0,
                                             base=-64, channel_multiplier=1), ["ones_f"], ["sgn"])
        op("pool", lambda e: e.iota(pidx[:], [[-128, 19]], base=384, channel_multiplier=1,
                                    allow_small_or_imprecise_dtypes=True), [], ["pidx"])
        for h in range(4):
            pts(biasd[:, h, :], pidx[:], SLOPES[4 + h], ["pidx"], ["biasd"])
        memset("pool", ft[0][:, 0:128], 0.0, ["ft0"])
        op("pool", lambda e: e.affine_select(ft[0][:, 0:128], ft[0][:, 0:128], [[1, 128]], ALU.is_ge,
                                             -240000.0, base=0, channel_multiplier=-1), ["ft0"], ["ft0"])
        cp("pool", mask_d[:], ft[0][:, 0:128], ["ft0"], ["mask_d"])
        for h in range(4):
            sl8 = 8.0 * SLOPES[h]
            op("pool", lambda e: e.iota(ft[1][:, 0:128], [[1, 128]], base=128, channel_multiplier=-1,
                                        allow_small_or_imprecise_dtypes=True), [], ["ft1"])
            pts(ft[1][:, 0:128], ft[1][:, 0:128], -sl8, ["ft1"], ["ft1"])
            op("pool", lambda e: e.affine_select(ft[1][:, 0:128], ft[1][:, 0:128], [[-1, 128]], ALU.is_ge,
                                                 -240000.0, base=-1, channel_multiplier=1), ["ft1"], ["ft1"])
            cp("pool", swab[:, h, 0:128], ft[1][:, 0:128], ["ft1"], ["swab"])
            op("pool", lambda e: e.iota(ft[1][:, 128:256], [[1, 128]], base=0, channel_multiplier=-1,
                                        allow_small_or_imprecise_dtypes=True), [], ["ft1"])
            pts(ft[1][:, 128:256], ft[1][:, 128:256], -sl8, ["ft1"], ["ft1"])
            op("pool", lambda e: e.affine_select(ft[1][:, 128:256], ft[1][:, 128:256], [[1, 128]], ALU.is_ge,
                                                 -240000.0, base=0, channel_multiplier=-1), ["ft1"], ["ft1"])
            cp("pool", swab[:, h, 128:256], ft[1][:, 128:256], ["ft1"], ["swab"])
        memset("pool", ssmw[:], 0.0, ["ssmw2", "ssmw3"] + [f"ssmw{k}_{g}" for k in range(2) for g in range(8)])
        op("pool", lambda e: e.affine_select(psw[:], ones_f[:], [[-1, 128]], ALU.is_equal, 0.0,
                                             base=-64, channel_multiplier=1), ["ones_f"], ["psw"])
        op("pool", lambda e: e.affine_select(ft[2][:, 0:128], ones_f[:], [[-1, 128]], ALU.is_equal, 0.0,
                                             base=64, channel_multiplier=1), ["ones_f"], ["ft2"])
        tt("pool", psw[:], psw[:], ft[2][:, 0:128], ALU.add, ["psw", "ft2"], ["psw"])

    def small_loads():
        for b in range(NSEQ):
            dma(condT[:, :, b], c_d[b].rearrange("(j p) -> p j", p=128), [], ["condT"], slow=True)
        dma(adabT[:], adab_d.rearrange("l (j p) -> p l j", p=128), [], ["adabT"], slow=True)
        dma(ngT[:], ng_d.rearrange("l (j p) -> p l j", p=128), [], ["ngT"], slow=True)
        dma(ssmdT[:], ssmd_d.rearrange("l (j p) -> p l j", p=128), [], ["ssmdT"], slow=True)
        dma(glubT[:], glub_d.rearrange("l (j p) -> p l j", p=128), [], ["glubT"], slow=True)
        dma(sublnT[:], subln_d.rearrange("l p -> p l"), [], ["sublnT"], slow=True)
        for l in range(L):
            for kvg in range(2):
                for a in range(2):
                    dma(sinkc[a * 64:(a + 1) * 64, l, kvg:kvg + 1],
                        sinks_d[l, 2 * kvg + a:2 * kvg + a + 1].partition_broadcast(64),
                        [], ["sinkc"], slow=True)
        act(sinkc[:], sinkc[:], AF.Exp, ["sinkc"], ["sinkc"])
        act(condT[:], condT[:], AF.Silu, ["condT"], ["condT"])

    def ada_mod():
        stg = [(wstage[0][:].rearrange("p c n -> p (c n)"), ["wstage0"]),
               (wstage[1][:].rearrange("p c n -> p (c n)"), ["wstage1"]),
               ftview(0), ftview(2), ftview(4)]
        rows, rowsn = ftview(6)
        k = 0
        for l in range(L):
            for j in range(3):
                for c in range(8):
                    sv, sn = stg[k % len(stg)]
                    k += 1
                    dma(sv, adaw_d[l, c * 128:(c + 1) * 128, j * 1024:(j + 1) * 1024], [], sn)
                    for hf in range(2):
                        mm(psb[hf][0:NSEQ, :], condT[:, c, :], sv[:, hf * 512:(hf + 1) * 512],
                           sn + ["condT"], [f"ps{hf}"], start=(c == 0), stop=(c == 7))
                dma(rows[0:NSEQ, :], adab_d[l, j * 1024:(j + 1) * 1024].partition_broadcast(NSEQ), [], rowsn,
                    slow=True)
                for hf in range(2):
                    tt("dve", rows[0:NSEQ, hf * 512:(hf + 1) * 512], psb[hf][0:NSEQ, :],
                       rows[0:NSEQ, hf * 512:(hf + 1) * 512], ALU.add, [f"ps{hf}"] + rowsn, rowsn)
                if j == 2:
                    dma(modscr_d[l], rows[0:NSEQ, :], rowsn, ["modscr"])
                else:
                    for c in range(8):
                        tr(psb[2][:, c * NSEQ:(c + 1) * NSEQ], rows[0:NSEQ, c * 128:(c + 1) * 128],
                           ident_f[0:NSEQ, 0:NSEQ], rowsn + ["ident_f"], ["ps2"], gend=(c == 7))
                    cp("dve", modT[:, l, j * 8:(j + 1) * 8, :],
                       psb[2][:, 0:8 * NSEQ].rearrange("p (c b) -> p c b", c=8), ["ps2"], ["modT"])
            for j in range(8):
                ts("dve", gmodT[:, l, j, :], modT[:, l, 8 + j, :], 1.0, ngT[:, l, j:j + 1],
                   ALU.add, ALU.mult, ["modT", "ngT"], ["gmodT"])

    wstate = {"n": 0}

    def load_wtile(srcs):
        i = wstate["n"]
        wstate["n"] += 1
        st = i % NSTG
        wb = i % NWB
        for ap, c0, ncol in srcs:
            dma(wstage[st][:, :, c0:c0 + ncol], ap.rearrange("(c p) n -> p c n", p=128),
                [], [f"wstage{st}"])
        cp("pool", wbf[wb][:], wstage[st][:], [f"wstage{st}"], [f"wbf{wb}"])
        return wb

    evac_rr = {"n": 0}

    def rstd_col(src_col, dst_col, n):
        ts("dve", dst_col, src_col, 1.0 / n, EPS, ALU.mult, ALU.add, ["small"], ["small"])
        act(dst_col, dst_col, AF.Ln, ["small"], ["small"])
        act(dst_col, dst_col, AF.Exp, ["small"], ["small"], scale=-0.5)

    def norm_to_hT(l, b):
        junk, junkn = ftview(0)
        for t in range(NT):
            par = t % 2
            xsv, xsn = ftview(2 + 2 * par)
            sm = f"small{par}"
            col = small[:, 2 * par:2 * par + 1]
            col2 = small[:, 2 * par + 1:2 * par + 2]
            act(junk, xres[:, t, :], AF.Square, [f"xres{t}"], junkn + [sm], accum=col)
            ts("dve", col2, col, 1.0 / D, EPS, ALU.mult, ALU.add, [sm], [sm])
            act(col2, col2, AF.Ln, [sm], [sm])
            act(col2, col2, AF.Exp, [sm], [sm], scale=-0.5)
            ts("dve", xsv, xres[:, t, :], col2, None, ALU.mult, None, [f"xres{t}", sm], xsn)
            for half in range(2):
                pb = 4 + 2 * par + half
                for q in range(4):
                    c = half * 4 + q
                    tr(psb[pb][:, q * 128:(q + 1) * 128], xsv[:, c * 128:(c + 1) * 128], ident_f[:],
                       xsn + ["ident_f"], [f"ps{pb}"], gend=(q == 3))
                for q in range(4):
                    c = half * 4 + q
                    dst = hT[:, c, t * 128:(t + 1) * 128]
                    src = psb[pb][:, q * 128:(q + 1) * 128]
                    if half == 0:
                        ts("dve", dst, src, gmodT[:, l, c, b:b + 1], modT[:, l, c, b:b + 1],
                           ALU.mult, ALU.add, [f"ps{pb}", "gmodT", "modT"], [f"hT{c}"])
                    else:
                        act(dst, src, AF.Identity, [f"ps{pb}", "gmodT", "modT"], [f"hT{c}"],
                            bias=modT[:, l, c, b:b + 1], scale=gmodT[:, l, c, b:b + 1])
                    evac_rr["n"] += 1

    def evac_eng():
        evac_rr["n"] += 1
        return "dve" if evac_rr["n"] % 3 != 2 else "act"

    def proj_fm(wb, dst_tile, dst_name):
        eng = evac_eng()
        for tb in range(NB):
            pb = tb % 2
            for c in range(8):
                mm(psb[pb][:, :], wbf[wb][:, c, :], hT[:, c, tb * 512:(tb + 1) * 512],
                   [f"wbf{wb}", f"hT{c}"], [f"ps{pb}"], start=(c == 0), stop=(c == 7))
            cp(eng, dst_tile[:, tb * 512:(tb + 1) * 512], psb[pb][:, :], [f"ps{pb}"], [dst_name])

    def proj_tok(wb):
        eng = evac_eng()
        for t4 in range(NT // 4):
            pb = t4 % 2
            for q in range(4):
                t = t4 * 4 + q
                for c in range(8):
                    mm(psb[pb][:, q * 128:(q + 1) * 128], hT[:, c, t * 128:(t + 1) * 128], wbf[wb][:, c, :],
                       [f"wbf{wb}", f"hT{c}"], [f"ps{pb}"], start=(c == 0), stop=(c == 7))
            cp(eng, vtok[:, t4 * 4:(t4 + 1) * 4, :], psb[pb][:, :].rearrange("p (q n) -> p q n", q=4),
               [f"ps{pb}"], ["vtok"])

    def wcols(l, c0, n=128):
        return win_d[l, :, c0:c0 + n]

    def S(i):
        return sp_s[:, i, :]

    def bc16(ap):
        return ap.unsqueeze(2).to_broadcast([128, 16, 16])

    def v3(ap):
        return ap.rearrange("p (g h) -> p g h", g=16)

    def ssm_prep(l):
        R, W = ["sp_s"], ["sp_s"]
        zl = ft[0]
        dma(zl[0:16, 0:64], lamre_d[l], [], ["ft0"])
        dma(zl[0:16, 64:128], lamre_d[l], [], ["ft0"])
        dma(zl[0:16, 128:192], lamim_d[l], [], ["ft0"])
        dma(zl[0:16, 192:256], lamim_d[l], [], ["ft0"])
        tr(psb[0][:, 0:16], zl[0:16, 0:128], ident_f[0:16, 0:16], ["ft0", "ident_f"], ["ps0"])
        tr(psb[0][:, 16:32], zl[0:16, 128:256], ident_f[0:16, 0:16], ["ft0", "ident_f"], ["ps0"])
        cp("dve", S(0), psb[0][:, 0:16], ["ps0"], W)
        cp("dve", S(1), psb[0][:, 16:32], ["ps0"], W)
        dma(S(2), lstep_d[l].partition_broadcast(128), [], W, slow=True)
        act(S(3), S(2), AF.Exp, R, W)
        tt("dve", S(4), S(0), S(3), ALU.mult, R, W)
        tt("dve", S(5), S(1), S(3), ALU.mult, R, W)
        act(rho[:], S(4), AF.Exp, R, ["rho"])
        ts("dve", S(6), S(5), 1.0 / TWO_PI, None, ALU.mult, None, R, W)
        cp("dve", itile[:, 0, 0:16], S(6), R, ["itile"])
        tt("dve", frq[:], S(6), itile[:, 0, 0:16], ALU.subtract, R + ["itile"], ["frq"])
        ts("dve", itile[:, 0, 16:32], S(6), 0.25, None, ALU.add, None, R, ["itile"])
        tt("dve", S(7), S(6), itile[:, 0, 16:32], ALU.subtract, R + ["itile"], W)
        act(S(8), frq[:], AF.Sin, ["frq"], W, scale=SC2PI)
        act(S(9), S(7), AF.Sin, R, W, scale=SC2PI, bias=HALFPI)
        tt("dve", S(10), rho[:], S(9), ALU.mult, R + ["rho"], W)
        tt("dve", S(11), rho[:], S(8), ALU.mult, R + ["rho"], W)
        ts("dve", S(12), S(10), -1.0, None, ALU.add, None, R, W)
        tt("dve", S(13), S(0), S(0), ALU.mult, R, W)
        tt("dve", S(14), S(1), S(1), ALU.mult, R, W)
        tt("dve", S(13), S(13), S(14), ALU.add, R, W)
        op("dve", lambda e: e.reciprocal(S(13), S(13)), R, W)
        tt("dve", S(14), S(12), S(0), ALU.mult, R, W)
        tt("dve", S(15), S(11), S(1), ALU.mult, R, W)
        tt("dve", S(14), S(14), S(15), ALU.add, R, W)
        tt("dve", S(14), S(14), S(13), ALU.mult, R, W)
        tt("dve", S(15), S(11), S(0), ALU.mult, R, W)
        tt("dve", S(16), S(12), S(1), ALU.mult, R, W)
        tt("dve", S(15), S(15), S(16), ALU.subtract, R, W)
        tt("dve", S(15), S(15), S(13), ALU.mult, R, W)
        ts("dve", S(16), S(15), sgn[:, 0:1], None, ALU.mult, None, R + ["sgn"], W)
        ts("dve", S(17), S(14), sgn[:, 0:1], -1.0, ALU.mult, ALU.mult, R + ["sgn"], W)
        ts("dve", S(7), frq[:], 512.0, None, ALU.mult, None, ["frq"], W)
        cp("dve", itile[:, 0, 0:16], S(7), R, ["itile"])
        tt("dve", S(19), S(7), itile[:, 0, 0:16], ALU.subtract, R + ["itile"], W)
        ts("dve", itile[:, 0, 16:32], S(7), 0.25, None, ALU.add, None, R, ["itile"])
        tt("dve", S(18), S(7), itile[:, 0, 16:32], ALU.subtract, R + ["itile"], W)
        act(S(19), S(19), AF.Sin, R, W, scale=SC2PI)
        act(S(18), S(18), AF.Sin, R, W, scale=SC2PI, bias=HALFPI)
        ts("dve", S(19), S(19), sgn[:, 0:1], None, ALU.mult, None, R + ["sgn"], W)
        X1 = v3(ft[1][:, 0:256])
        X2 = v3(ft[2][:, 0:256])
        T = v3(ft[3][:, 0:256])
        dma(X1[0:64], bre_d[l].rearrange("g p h -> p g h"), [], ["ft1"], slow=True)
        dma(X1[64:128], bim_d[l].rearrange("g p h -> p g h"), [], ["ft1"], slow=True)
        dma(X2[0:64], bim_d[l].rearrange("g p h -> p g h"), [], ["ft2"], slow=True)
        dma(X2[64:128], bre_d[l].rearrange("g p h -> p g h"), [], ["ft2"], slow=True)
        tt("dve", T, X1, bc16(S(14)), ALU.mult, ["ft1", "sp_s"], ["ft3"])
        tt("dve", sp_a[:], X2, bc16(S(16)), ALU.mult, ["ft2", "sp_s"], ["sp_a"])
        tt("dve", sp_a[:], sp_a[:], T, ALU.add, ["sp_a", "ft3"], ["sp_a"])
        tt("dve", T, X2, bc16(S(17)), ALU.mult, ["ft2", "sp_s"], ["ft3"])
        tt("dve", sp_b[:], X1, bc16(S(15)), ALU.mult, ["ft1", "sp_s"], ["sp_b"])
        tt("dve", sp_b[:], sp_b[:], T, ALU.add, ["sp_b", "ft3"], ["sp_b"])
        for mc in range(2):
            Z1 = ft[4]
            Z2 = ft[5]
            cr = cre_d[l, mc * 8:(mc + 1) * 8].rearrange("g h p -> (g h) p")
            ci = cim_d[l, mc * 8:(mc + 1) * 8].rearrange("g h p -> (g h) p")
            dma(Z1[:, 0:64], cr, [], ["ft4"])
            dma(Z1[:, 64:128], ci, [], ["ft4"])
            dma(Z2[:, 0:64], ci, [], ["ft5"])
            dma(Z2[:, 64:128], cr, [], ["ft5"])
            tr(psb[1][:, 0:128], Z1[:, 0:128], ident_f[:], ["ft4", "ident_f"], ["ps1"])
            tr(psb[1][:, 128:256], Z2[:, 0:128], ident_f[:], ["ft5", "ident_f"], ["ps1"])
            cp("dve", sp_c[:, mc * 8:(mc + 1) * 8, :], psb[1][:, 0:128].rearrange("p (g h) -> p g h", g=8),
               ["ps1"], ["sp_c"])
            cp("dve", sp_d[:, mc * 8:(mc + 1) * 8, :], psb[1][:, 128:256].rearrange("p (g h) -> p g h", g=8),
               ["ps1"], ["sp_d"])
        gst = ft[3].rearrange("p (k n) -> p k n", k=2)
        dma(gst, gluw_d[l].rearrange("(k p) n -> p k n", p=128), [], ["ft3"])
        cp("pool", gluw_sb[:], gst, ["ft3"], ["gluw_sb"])

    def ssm_weights(mc):
        s1 = sp_a[:, mc * 8:(mc + 1) * 8, :].rearrange("p g h -> p (g h)")
        s2 = sp_b[:, mc * 8:(mc + 1) * 8, :].rearrange("p g h -> p (g h)")
        tr(psb[0][:, 0:128], s1, ident_f[:], ["sp_a", "ident_f"], ["ps0"])
        tr(psb[1][:, 128:256], s2, ident_f[:], ["sp_b", "ident_f"], ["ps1"])
        for gi in range(8):
            ts("dve", ssmw[:, gi, 0, :], psb[0][:, 0:128], rowmask[:, gi:gi + 1], None, ALU.mult, None,
               ["ps0", "rowmask"], [f"ssmw0_{gi}"])
            act(ssmw[:, gi, 1, :], psb[1][:, 128:256], AF.Identity, ["ps1", "rowmask"], [f"ssmw1_{gi}"],
                scale=rowmask[:, gi:gi + 1])
        d1 = bass.AP(ssmw, 2 * 128, [[8 * 4 * 128, 128], [4 * 128 + 16, 8], [1, 16]])
        d2 = bass.AP(ssmw, 3 * 128, [[8 * 4 * 128, 128], [4 * 128 + 16, 8], [1, 16]])
        ts("dve", d1, sp_c[:, mc * 8:(mc + 1) * 8, :], sgn[:, 0:1], -1.0, ALU.mult, ALU.mult,
           ["sp_c", "sgn"], ["ssmw2"])
        ts("dve", d2, sp_d[:, mc * 8:(mc + 1) * 8, :], -1.0, None, ALU.mult, None, ["sp_d"], ["ssmw3"])

    def ssm_phase(l, b):
        ssm_prep(l)
        for mc in range(2):
            wb = load_wtile([(wcols(l, mc * 128), 0, 128)])
            proj_fm(wb, fm[mc], f"fm{mc}")
            ssm_weights(mc)
            def gen_tables(gi):
                g = mc * 8 + gi
                k = gi % 2
                A, Bt = ft[k * 2], ft[k * 2 + 1]
                An, Bn = f"ft{k * 2}", f"ft{k * 2 + 1}"
                I0 = itile[:, 0, :]
                In = "itile"
                op("pool", lambda e, A=A: e.iota(A, [[1, 512]], base=0, channel_multiplier=0,
                                                 allow_small_or_imprecise_dtypes=True), [], [An])
                pts(A, A, frq[:, g:g + 1], [An, "frq"], [An])
                cp("pool", I0, A, [An], [In])
                tt("pool", Bt, A, I0, ALU.subtract, [An, In], [Bn])
                pts(I0, A, 0.25, [An], [In], s2=1.0, op0=ALU.add, op1=ALU.mult)
                tt("pool", A, A, I0, ALU.subtract, [An, In], [An])
                act(Bt, Bt, AF.Sin, [Bn], [Bn], scale=SC2PI)
                act(A, A, AF.Sin, [An], [An], scale=SC2PI, bias=HALFPI)

            gen_tables(0)
            iters = [(gi, tb) for gi in range(8) for tb in range(NB)]

            def bufs_of(i):
                gi, tb = iters[i]
                k = gi % 2
                kk = i % 2
                return dict(gi=gi, tb=tb, g=mc * 8 + gi, kk=kk,
                            A=ft[k * 2], Bt=ft[k * 2 + 1], An=f"ft{k * 2}", Bn=f"ft{k * 2 + 1}",
                            T1=ft[4 + kk], T2=ft[6 + kk], T1n=f"ft{4 + kk}", T2n=f"ft{6 + kk}",
                            P1=bt[kk * 2][:], P2=bt[kk * 2 + 1][:], P1n=f"bt{kk * 2}", P2n=f"bt{kk * 2 + 1}",
                            cols=slice(tb * 512, (tb + 1) * 512))

            def stage_bu(i):
                d = bufs_of(i)
                pa = d["kk"]
                mm(psb[pa][:, :], ssmw[:, d["gi"], 0, :], fm[mc][:, d["cols"]],
                   [f"ssmw0_{d['gi']}", f"fm{mc}"], [f"ps{pa}"])
                mm(psb[3][:, :], ssmw[:, d["gi"], 1, :], fm[mc][:, d["cols"]],
                   [f"ssmw1_{d['gi']}", f"fm{mc}"], ["ps3"])

            def stage_dve(i):
                d = bufs_of(i)
                pa, g, tb = d["kk"], d["g"], d["tb"]
                A, Bt, T1, T2, P1, P2 = d["A"], d["Bt"], d["T1"], d["T2"], d["P1"], d["P2"]
                An, Bn, T1n, T2n, P1n, P2n = d["An"], d["Bn"], d["T1n"], d["T2n"], d["P1n"], d["P2n"]
                tt("dve", T1, psb[pa][:, :], A, ALU.mult, [f"ps{pa}", An], [T1n])
                tt("dve", T2, psb[3][:, :], Bt, ALU.mult, ["ps3", Bn], [T2n])
                tt("dve", T1, T1, T2, ALU.add, [T1n, T2n], [T1n])
                if tb == 0:
                    init = 0.0
                else:
                    mm(psb[2][:, 0:1], psw[:], carry[:, g:g + 1], ["psw", "carry"], ["ps2"])
                    ts("dve", small[:, 20:21], carry[:, g:g + 1], sp_s[:, 18, g:g + 1], None, ALU.mult, None,
                       ["carry", "sp_s"], ["small"])
                    stt(small[:, 21:22], psb[2][:, 0:1], sp_s[:, 19, g:g + 1], small[:, 20:21],
                        ALU.mult, ALU.add, ["ps2", "sp_s", "small"], ["small"])
                    init = small[:, 21:22]
                op("dve", lambda e, T2=T2, T1=T1, g=g, init=init: e.tensor_tensor_scan(
                    T2, rho[:, g:g + 1].to_broadcast([128, 512]), T1, init, ALU.mult, ALU.add),
                   [T1n, "rho", "small"], [T2n])
                if tb < NB - 1:
                    cp("act", carry[:, g:g + 1], T2[:, 511:512], [T2n], ["carry"])
                tt("pool", P1, A, T2, ALU.mult, [An, T2n], [P1n])
                tt("dve", P2, Bt, T2, ALU.mult, [Bn, T2n], [P2n])

            def stage_y(i):
                d = bufs_of(i)
                gi, tb = d["gi"], d["tb"]
                mm(psb[4 + tb][:, :], ssmw[:, gi, 2, :], d["P1"], ["ssmw2", d["P1n"]], [f"ps{4 + tb}"],
                   start=(gi == 0), stop=False)
                mm(psb[4 + tb][:, :], ssmw[:, gi, 3, :], d["P2"], ["ssmw3", d["P2n"]], [f"ps{4 + tb}"],
                   start=False, stop=(gi == 7))

            for i in range(len(iters)):
                stage_bu(i)
                stage_dve(i)
                if i > 0:
                    stage_y(i - 1)
                gi, tb = iters[i]
                if tb == 0 and gi < 7:
                    gen_tables(gi + 1)
            stage_y(len(iters) - 1)
            for tb in range(NB):
                cols = slice(tb * 512, (tb + 1) * 512)
                Y, X2_ = ft[4 + tb % 2], ft[6 + tb % 2]
                Yn, Xn = f"ft{4 + tb % 2}", f"ft{6 + tb % 2}"
                stt(Y, fm[mc][:, cols], ssmdT[:, l, mc:mc + 1], psb[4 + tb][:, :], ALU.mult, ALU.add,
                    [f"fm{mc}", "ssmdT", f"ps{4 + tb}"], [Yn])
                act(X2_, Y, AF.Square, [Yn], [Xn])
                ts("dve", X2_, X2_, 0.044715, 1.0, ALU.mult, ALU.add, [Xn], [Xn])
                tt("dve", X2_, X2_, Y, ALU.mult, [Xn, Yn], [Xn])
                act(X2_, X2_, AF.Sigmoid, [Xn], [Xn], scale=1.5957691216)
                tt("dve", mixT[:, mc, cols], Y, X2_, ALU.mult, [Yn, Xn], ["mixT"])
        for nt in range(2):
            for tb in range(NB):
                cols = slice(tb * 512, (tb + 1) * 512)
                pb = tb % 2
                for k in range(2):
                    mm(psb[pb][:, :], gluw_sb[:, k, nt * 128:(nt + 1) * 128], mixT[:, k, cols],
                       ["gluw_sb", "mixT"], [f"ps{pb}"], start=(k == 0), stop=(k == 1))
                act(fm[1 + nt][:, cols], psb[pb][:, :], AF.Sigmoid, [f"ps{pb}", "glubT"], [f"fm{1 + nt}"],
                    bias=glubT[:, l, nt:nt + 1])
        for nt in range(2):
            wb = load_wtile([(wcols(l, 256 + nt * 128), 0, 128)])
            proj_fm(wb, fm[0], "fm0")
            act(fm[0][:, :], fm[0][:, :], AF.Silu, ["fm0"], ["fm0"])
            tt("dve", fm[1 + nt][:, :], fm[1 + nt][:, :], fm[0][:, :], ALU.mult, [f"fm{1 + nt}", "fm0"],
               [f"fm{1 + nt}"])
            tt("dve", mixT[:, nt, :], mixT[:, nt, :], fm[1 + nt][:, :], ALU.mult, ["mixT", f"fm{1 + nt}"],
               ["mixT"])

    NLAM, GCOL = small[:, 8:9], small[:, 9:10]

    def diff_prep(l):
        lam_init = 0.8 - 0.6 * math.exp(-0.3 * l)
        memset("pool", fm[1][64:128, :], 0.0, ["fm1"])
        memset("pool", iraw[0:64, :], 0.0, ["iraw"])
        lamb = ft[5][:, 0:256].rearrange("p (i n) -> p i n", i=4)
        for i, d_ in enumerate((lq1_d, lk1_d, lq2_d, lk2_d)):
            dma(lamb[:, i, :], d_[l].partition_broadcast(128), [], ["ft5"], slow=True)
        tt("dve", lamb[:, 0, :], lamb[:, 0, :], lamb[:, 1, :], ALU.mult, ["ft5"], ["ft5"])
        tt("dve", lamb[:, 2, :], lamb[:, 2, :], lamb[:, 3, :], ALU.mult, ["ft5"], ["ft5"])
        op("dve", lambda e: e.reduce_sum(small[:, 10:11], lamb[:, 0, :], AX.X), ["ft5"], ["small"])
        op("dve", lambda e: e.reduce_sum(small[:, 11:12], lamb[:, 2, :], AX.X), ["ft5"], ["small"])
        act(small[:, 10:12], small[:, 10:12], AF.Exp, ["small"], ["small"])
        tt("dve", NLAM, small[:, 11:12], small[:, 10:11], ALU.subtract, ["small"], ["small"])
        ts("dve", NLAM, NLAM, -lam_init, None, ALU.add, None, ["small"], ["small"])
        ts("dve", GCOL, sublnT[:, l:l + 1], 1.0 - lam_init, None, ALU.mult, None, ["sublnT"], ["small"])

    def diff_head(l, h):
        wq = load_wtile([(wcols(l, 512 + 128 * h), 0, 128)])
        proj_fm(wq, fm[0], "fm0")
        wk = load_wtile([(wcols(l, 1024 + 128 * h), 0, 128)])
        for tb in range(NB):
            pb = tb % 2
            for c in range(8):
                mm(psb[pb][:, :], wbf[wk][:, c, :], hT[:, c, tb * 512:(tb + 1) * 512],
                   [f"wbf{wk}", f"hT{c}"], [f"ps{pb}"], start=(c == 0), stop=(c == 7))
            cs_ = slice(tb * 512, (tb + 1) * 512)
            cp("dve", fm[1][0:64, cs_], psb[pb][0:64, :], [f"ps{pb}"], ["fm1"])
            cp("dve", iraw[64:128, cs_], psb[pb][64:128, :], [f"ps{pb}"], ["iraw"])
        wv = load_wtile([(wcols(l, 1536 + 128 * h), 0, 128)])
        proj_tok(wv)
        wz = load_wtile([(wcols(l, 2048 + 128 * h), 0, 128)])
        proj_fm(wz, fm[2], "fm2")
        act(fm[2][:, :], fm[2][:, :], AF.Silu, ["fm2"], ["fm2"])
        tiles = []
        for qc in range(NB):
            nkb = 4 * qc + 4
            for a in range(2):
                for kb in range(nkb):
                    tiles.append((qc, a, kb, nkb))
        LA = 2
        SB = (1, 2, 3)

        def emit_qk(i):
            qc, a, kb, nkb = tiles[i]
            rows = slice(a * 64, (a + 1) * 64)
            j = kb - 4 * qc
            c0 = 128 * j if j >= 0 else 0
            sb_ = SB[i % 3]
            kt = fm[1] if a == 0 else iraw
            mm(psb[sb_][:, c0:512], kt[:, kb * 128:(kb + 1) * 128],
               fm[0][:, qc * 512 + c0:(qc + 1) * 512], ["fm0", "fm1", "itile", "itile0", "itile1"], [f"ps{sb_}"],
               start=True, stop=(j < 0))
            if j >= 0:
                mm(psb[sb_][:, c0:c0 + 128], ident_b[:], mask_d[:], ["ident_b", "mask_d"], [f"ps{sb_}"],
                   start=False, stop=True)

        def emit_rest(i):
            qc, a, kb, nkb = tiles[i]
            j = kb - 4 * qc
            c0 = 128 * j if j >= 0 else 0
            sb_ = SB[i % 3]
            accb, sumb = 4 + a, 6 + a
            PT = bt[i % NBT]
            PTn = f"bt{i % NBT}"
            di = 4 * qc - kb + 3
            act(PT[:, c0:512], psb[sb_][:, c0:512], AF.Exp, [f"ps{sb_}", "biasd"], [PTn],
                bias=biasd[:, h, di:di + 1], scale=0.125)
            mm(psb[accb][:, c0:512], vtok[:, kb, :], PT[:, c0:512], ["vtok", PTn], [f"ps{accb}"],
               start=(kb == 0), stop=(kb == nkb - 1))
            mm(psb[sumb][:, c0:512], ones_b[:], PT[:, c0:512], ["ones_b", PTn], [f"ps{sumb}"],
               start=(kb == 0), stop=(kb == nkb - 1))

        def epilogue(qc):
            cols = slice(qc * 512, (qc + 1) * 512)
            r1, r2, o1, o2, sq = ft[0], ft[1], ft[2], ft[3], ft[4]
            act(r1, psb[6][:, :], AF.Ln, ["ps6"], ["ft0"])
            act(r2, psb[7][:, :], AF.Ln, ["ps7"], ["ft1"])
            cp("dve", o1, psb[4][:, :], ["ps4"], ["ft2"])
            cp("dve", o2, psb[5][:, :], ["ps5"], ["ft3"])
            steps = [
                lambda: act(r1, r1, AF.Exp, ["ft0"], ["ft0"], scale=-1.0),
                lambda: act(r2, r2, AF.Exp, ["ft1"], ["ft1"], scale=-1.0),
                lambda: tt("dve", o1, o1, r1, ALU.mult, ["ft2", "ft0"], ["ft2"]),
                lambda: tt("dve", o2, o2, r2, ALU.mult, ["ft3", "ft1"], ["ft3"]),
                lambda: stt(o1, o2, NLAM, o1, ALU.mult, ALU.add, ["ft3", "small", "ft2"], ["ft2"]),
                lambda: act(sq, o1, AF.Square, ["ft2"], ["ft4"]),
                lambda: None,
                lambda: mm(psb[0][:, :], ones_f[:], sq, ["ones_f", "ft4"], ["ps0"]),
                lambda: None,
                lambda: ts("dve", r1, psb[0][:, :], 1.0 / 128.0, EPS, ALU.mult, ALU.add, ["ps0"], ["ft0"]),
                lambda: act(r1, r1, AF.Ln, ["ft0"], ["ft0"]),
                lambda: act(r1, r1, AF.Exp, ["ft0"], ["ft0"], scale=-0.5),
                lambda: tt("dve", o1, o1, r1, ALU.mult, ["ft2", "ft0"], ["ft2"]),
                lambda: stt(mixT[:, 2 + h, cols], o1, GCOL, fm[2][:, cols], ALU.mult, ALU.mult,
                            ["ft2", "small", "fm2"], ["mixT"]),
            ]
            return steps

        n = len(tiles)
        pending = []
        for i in range(min(LA, n)):
            emit_qk(i)
        for i in range(n):
            emit_rest(i)
            if i + LA < n:
                emit_qk(i + LA)
            if pending:
                pending.pop(0)()
            qc, a, kb, nkb = tiles[i]
            if a == 1 and kb == nkb - 1:
                while pending:
                    pending.pop(0)()
                pending = epilogue(qc)
        while pending:
            pending.pop(0)()

    def swa_phase(l, b):
        wv = load_wtile([(wcols(l, 2944), 0, 128)])
        proj_tok(wv)
        pt_i = 0
        for kvg in range(2):
            wq = load_wtile([(wcols(l, 2560 + 128 * kvg), 0, 128)])
            proj_fm(wq, fm[0], "fm0")
            wk = load_wtile([(wcols(l, 2816 + 64 * kvg, 64), 0, 64), (wcols(l, 2816 + 64 * kvg, 64), 64, 64)])
            proj_fm(wk, fm[1], "fm1")
            wz = load_wtile([(wcols(l, 3072 + 128 * kvg), 0, 128)])
            proj_fm(wz, fm[2], "fm2")
            if kvg == 1 and "out" in PHASES:
                out_proj_load(l, b)
            act(fm[2][:, :], fm[2][:, :], AF.Silu, ["fm2"], ["fm2"])
            vc = slice(kvg * 64, (kvg + 1) * 64)

            def emit_qk(n):
                qcols = slice(n * 128, (n + 1) * 128)
                for a in range(2):
                    hh = 2 * kvg + a
                    rows = slice(a * 64, (a + 1) * 64)
                    sb_ = 2 * (n % 2) + a
                    sn = f"ps{sb_}"
                    if n > 0:
                        mm(psb[sb_][:, 0:128], fm[1][rows, (n - 1) * 128:n * 128], fm[0][rows, qcols],
                           ["fm0", "fm1"], [sn], start=True, stop=False)
                        mm(psb[sb_][:, 0:128], ident_b[:], swab[:, hh, 0:128], ["ident_b", "swab"],
                           [sn], start=False, stop=True)
                    mm(psb[sb_][:, 128:256], fm[1][rows, qcols], fm[0][rows, qcols],
                       ["fm0", "fm1"], [sn], start=True, stop=False)
                    mm(psb[sb_][:, 128:256], ident_b[:], swab[:, hh, 128:256], ["ident_b", "swab"],
                       [sn], start=False, stop=True)

            def emit_rest(n, PT, PTn):
                q = n % 4
                par = (n // 4) % 2
                accb, sumb = 4 + 2 * par, 5 + 2 * par
                oc = slice(q * 128, (q + 1) * 128)
                for a in range(2):
                    rows = slice(a * 64, (a + 1) * 64)
                    sb_ = 2 * (n % 2) + a
                    sn = f"ps{sb_}"
                    lo = 0 if n > 0 else 128
                    act(PT[:, a * 256 + lo:a * 256 + 256], psb[sb_][:, lo:256], AF.Exp,
                        [sn], [PTn], scale=0.125)
                    if n > 0:
                        mm(psb[accb][rows, oc], vtok[:, n - 1, vc], PT[:, a * 256:a * 256 + 128],
                           ["vtok", PTn], [f"ps{accb}"], start=True, stop=False)
                        mm(psb[sumb][rows, oc], ones_b[:, 0:64], PT[:, a * 256:a * 256 + 128],
                           ["ones_b", PTn], [f"ps{sumb}"], start=True, stop=False)
                    mm(psb[accb][rows, oc], vtok[:, n, vc], PT[:, a * 256 + 128:a * 256 + 256],
                       ["vtok", PTn], [f"ps{accb}"], start=(n == 0), stop=True)
                    mm(psb[sumb][rows, oc], ones_b[:, 0:64], PT[:, a * 256 + 128:a * 256 + 256],
                       ["ones_b", PTn], [f"ps{sumb}"], start=(n == 0), stop=True)

            def epilogue(n4):
                par = n4 % 2
                accb, sumb = 4 + 2 * par, 5 + 2 * par
                cols = slice(n4 * 512, (n4 + 1) * 512)
                den, o_ = ft[2 * par], ft[2 * par + 1]
                dn, on_ = f"ft{2 * par}", f"ft{2 * par + 1}"
                act(den, psb[sumb][:, :], AF.Ln, [f"ps{sumb}", "sinkc"], [dn], bias=sinkc[:, l, kvg:kvg + 1])
                cp("dve", o_, psb[accb][:, :], [f"ps{accb}"], [on_])
                return [
                    lambda: act(den, den, AF.Exp, [dn], [dn], scale=-1.0),
                    lambda: tt("dve", o_, o_, den, ALU.mult, [on_, dn], [on_]),
                    lambda: tt("dve", mixT[:, 6 + kvg, cols], o_, fm[2][:, cols], ALU.mult, [on_, "fm2"],
                               ["mixT"]),
                ]

            pending = []
            emit_qk(0)
            for n in range(NT):
                if n + 1 < NT:
                    emit_qk(n + 1)
                PT = bt[pt_i % NBT]
                PTn = f"bt{pt_i % NBT}"
                pt_i += 1
                emit_rest(n, PT, PTn)
                if pending:
                    pending.pop(0)()
                if n % 4 == 3:
                    while pending:
                        pending.pop(0)()
                    pending = epilogue(n // 4)
            while pending:
                pending.pop(0)()

    def wout_store():
        if wout_bf is None:
            return [hT[:, k, 0:D] for k in range(8)], [f"hT{k}" for k in range(8)]
        return [wout_bf[:, k, :] for k in range(8)], [f"wout_bf{k}" for k in range(8)]

    def out_proj_load(l, b):
        dma(bc[:], modscr_d[l, b:b + 1, :].partition_broadcast(128).rearrange("p o n -> p (o n)"),
            ["modscr"], ["bc"])
        wst, wname = wout_store()
        for k in range(8):
            i = wstate["n"]
            wstate["n"] += 1
            st = i % NSTG
            stv = wstage[st][:].rearrange("p c n -> p (c n)")
            dma(stv, wout_d[l, k * 128:(k + 1) * 128, :], [], [f"wstage{st}"])
            tt("pool", wst[k], stv, bc[:], ALU.mult, [f"wstage{st}", "bc"], [wname[k]])

    def out_proj(l, b):
        wst, wname = wout_store()
        for t in range(NT):
            for half in range(2):
                pb = (t * 2 + half) % 2
                cs = slice(half * 512, (half + 1) * 512)
                for k in range(8):
                    mm(psb[pb][:, :], mixT[:, k, t * 128:(t + 1) * 128], wst[k][:, cs], ["mixT", wname[k]],
                       [f"ps{pb}"], start=(k == 0), stop=(k == 7))
                tt("dve", xres[:, t, cs], psb[pb][:, :], xres[:, t, cs], ALU.add, [f"ps{pb}", f"xres{t}"], [f"xres{t}"])

    def final_norm(b):
        dma(bc[:], fg_d.partition_broadcast(128), [], ["bc"])
        junk, junkn = ftview(0)
        for t in range(NT):
            par = t % 3
            xsv, xsn = ftview(2 + 2 * par)
            sm = f"small{par}"
            col = small[:, 2 * par:2 * par + 1]
            col2 = small[:, 2 * par + 1:2 * par + 2]
            act(junk, xres[:, t, :], AF.Square, [f"xres{t}"], junkn + [sm], accum=col)
            ts("dve", col2, col, 1.0 / D, EPS, ALU.mult, ALU.add, [sm], [sm])
            act(col2, col2, AF.Ln, [sm], [sm])
            act(col2, col2, AF.Exp, [sm], [sm], scale=-0.5)
            stt(xsv, xres[:, t, :], col2, bc[:], ALU.mult, ALU.mult, [f"xres{t}", sm, "bc"], xsn)
            dma(out_d[b, t * 128:(t + 1) * 128, :], xsv, xsn, ["outd"])

    consts()
    small_loads()
    for t in range(NT):
        dma(xres[:, t, :], x_d[0, t * 128:(t + 1) * 128, :], [], [f"xres{t}"])
    ada_mod()
    phases = PHASES

    for b in range(NSEQ):
        if b > 0:
            for t in range(NT):
                dma(xres[:, t, :], x_d[b, t * 128:(t + 1) * 128, :], [], [f"xres{t}"])
        for l in range(L):
            norm_to_hT(l, b)
            if "ssm" in phases:
                ssm_phase(l, b)
            if "diff" in phases:
                diff_prep(l)
                for h in range(4):
                    diff_head(l, h)
            if "swa" in phases:
                swa_phase(l, b)
            if dbg is not None and l == 0 and b == 0:
                for c in range(8):
                    cp("dve", ft[2], mixT[:, c, 0:512], ["mixT"], ["ft2"])
                    dma(dbg_d[c], ft[2], ["ft2"], ["dbgout"])
            if "out" in phases:
                if "swa" not in phases:
                    out_proj_load(l, b)
                out_proj(l, b)
        final_norm(b)

    sems = {}
    for e in ("pe", "act", "dve", "pool"):
        sems[e] = es.enter_context(nc.semaphore(f"s_{e}"))
    dsems = [es.enter_context(nc.semaphore(f"s_dma{i}")) for i in range(Prog.NDMA)]
    P.resolve(sems, dsems)
    with nc.Block() as block:
        @block.sync
        def _(e):
            P.emit("sp", e, final=True)

        @block.tensor
        def _(e):
            P.emit("pe", e)

        @block.scalar
        def _(e):
            P.emit("act", e)

        @block.vector
        def _(e):
            P.emit("dve", e)

        @block.gpsimd
        def _(e):
            P.emit("pool", e)
    es.close()
    return nc


INPUT_NAMES = ["x", "c", "norm_gain", "ada_w", "ada_b", "w_in", "w_out", "ssm_lam_re", "ssm_lam_im",
               "ssm_log_step", "ssm_b_re", "ssm_b_im", "ssm_c_re", "ssm_c_im", "ssm_d", "glu_w", "glu_b",
               "diff_lq1", "diff_lk1", "diff_lq2", "diff_lk2", "diff_subln", "swa_sinks", "final_gain"]


def run(inputs, ncores, dbg=None):
    x = np.ascontiguousarray(np.asarray(inputs["x"], dtype=np.float32))
    Bt, S, _ = x.shape
    nseq = Bt // ncores
    nc = build_nc(nseq, S, dbg=dbg)
    in_maps = []
    for i in range(ncores):
        m = {}
        for k in INPUT_NAMES:
            a = np.ascontiguousarray(np.asarray(inputs[k], dtype=np.float32))
            if k in ("x", "c"):
                a = np.ascontiguousarray(a[i * nseq:(i + 1) * nseq])
            m[k] = a
        in_maps.append(m)
    res = run_bass_kernel_spmd(nc, in_maps, core_ids=list(range(ncores)))
    out = np.concatenate([np.asarray(r["out"]) for r in res.results], axis=0)
    if dbg is not None:
        return out, [np.asarray(r["dbg"]) for r in res.results]
    return out


def kernel(**inputs):
    return run(inputs, NCORES).astype(np.float32)
```

```python
import math
from contextlib import ExitStack

import numpy as np
import concourse.bass as bass
import concourse.mybir as mybir
from concourse.bass_utils import run_bass_kernel_spmd

F32 = mybir.dt.float32
BF16 = mybir.dt.bfloat16
I32 = mybir.dt.int32
ALU = mybir.AluOpType
AF = mybir.ActivationFunctionType
AX = mybir.AxisListType

D = 1024
L = 2
NCORES = 8
PHASES = ("ssm", "diff", "swa", "out")
SBUF_LEFT = []
CHECK_PSUM = False
PSUM_WARN = []
IN_COLS = 3328
EPS = 1e-6
TWO_PI = 2.0 * math.pi
SLOPES = [2.0 ** (-(i + 1)) for i in range(8)]


class Buf:
    __slots__ = ("name", "w", "r")

    def __init__(self, name):
        self.name = name
        self.w = None
        self.r = []


class Op:
    __slots__ = ("eng", "fn", "idx", "deps", "signal", "gend", "sem", "val", "waits")

    def __init__(self, eng, fn, idx, gend):
        self.eng = eng
        self.fn = fn
        self.idx = idx
        self.deps = []
        self.signal = False
        self.gend = gend
        self.sem = None
        self.val = 0
        self.waits = []


class Prog:
    ENGS = ("pe", "act", "dve", "pool", "sp")
    NDMA = 16

    def __init__(self):
        self.ops = {e: [] for e in self.ENGS}

    def add(self, eng, fn, reads=(), writes=(), gend=True):
        lst = self.ops[eng]
        op = Op(eng, fn, len(lst), gend)
        deps = []
        for b in reads:
            if CHECK_PSUM and b.name.startswith("ps") and b.name[2:3].isdigit():
                for r in b.r:
                    if r.eng != eng:
                        PSUM_WARN.append((b.name, r.eng, eng))
            if b.w is not None:
                deps.append(b.w)
        for b in writes:
            if b.w is not None:
                deps.append(b.w)
            deps.extend(b.r)
        seen = set()
        for d in deps:
            if d.eng == "pe":
                if eng == "pe":
                    continue
                pl = self.ops["pe"]
                j = d.idx
                while j < len(pl) and not pl[j].gend:
                    j += 1
                if j < len(pl):
                    d = pl[j]
            if id(d) in seen:
                continue
            seen.add(id(d))
            d.signal = True
            op.deps.append(d)
        lst.append(op)
        for b in writes:
            b.w = op
            b.r = []
        for b in reads:
            b.r.append(op)
        return op

    def resolve(self, sems, dsems):
        for e in self.ENGS:
            if e == "sp":
                cnt = [0] * self.NDMA
                for i, op in enumerate(self.ops[e]):
                    k = i % self.NDMA
                    cnt[k] += 16
                    op.signal = True
                    op.sem = dsems[k]
                    op.val = cnt[k]
                self.dma_final = [(dsems[k], cnt[k]) for k in range(self.NDMA) if cnt[k]]
            else:
                n = 0
                for op in self.ops[e]:
                    if op.signal:
                        n += 1
                        op.sem = sems[e]
                        op.val = n
        for e in self.ENGS:
            waited = {}
            for op in self.ops[e]:
                need = {}
                for d in op.deps:
                    k = id(d.sem)
                    if d.val > need.get(k, (None, 0))[1]:
                        need[k] = (d.sem, d.val)
                for k, (s, v) in need.items():
                    if waited.get(k, 0) >= v:
                        continue
                    waited[k] = v
                    op.waits.append((s, v))

    def emit(self, eng_name, e, final=False):
        for op in self.ops[eng_name]:
            if eng_name == "sp" and op.val > 16:
                e.wait_ge(op.sem, op.val - 16)
            for s, v in op.waits:
                e.wait_ge(s, v)
            ins = op.fn(e)
            if op.signal:
                ins.then_inc(op.sem, 16 if eng_name == "sp" else 1)
        if final:
            for s, v in self.dma_final:
                e.wait_ge(s, v)


def build_nc(NSEQ, S, dbg=None):
    NT = S // 128
    NB = S // 512
    nc = bass.Bass("TRN2", target_bir_lowering=False)
    P = Prog()

    def dram_in(name, shape):
        return nc.dram_tensor(name, list(shape), F32, kind="ExternalInput").ap()

    x_d = dram_in("x", [NSEQ, S, D])
    c_d = dram_in("c", [NSEQ, D])
    ng_d = dram_in("norm_gain", [L, D])
    adaw_d = dram_in("ada_w", [L, D, 3 * D])
    adab_d = dram_in("ada_b", [L, 3 * D])
    win_d = dram_in("w_in", [L, D, IN_COLS])
    wout_d = dram_in("w_out", [L, D, D])
    lamre_d = dram_in("ssm_lam_re", [L, 16, 64])
    lamim_d = dram_in("ssm_lam_im", [L, 16, 64])
    lstep_d = dram_in("ssm_log_step", [L, 16])
    bre_d = dram_in("ssm_b_re", [L, 16, 64, 16])
    bim_d = dram_in("ssm_b_im", [L, 16, 64, 16])
    cre_d = dram_in("ssm_c_re", [L, 16, 16, 64])
    cim_d = dram_in("ssm_c_im", [L, 16, 16, 64])
    ssmd_d = dram_in("ssm_d", [L, 256])
    gluw_d = dram_in("glu_w", [L, 256, 256])
    glub_d = dram_in("glu_b", [L, 256])
    lq1_d = dram_in("diff_lq1", [L, 64])
    lk1_d = dram_in("diff_lk1", [L, 64])
    lq2_d = dram_in("diff_lq2", [L, 64])
    lk2_d = dram_in("diff_lk2", [L, 64])
    subln_d = dram_in("diff_subln", [L, 128])
    sinks_d = dram_in("swa_sinks", [L, 4])
    fg_d = dram_in("final_gain", [D])
    out_d = nc.dram_tensor("out", [NSEQ, S, D], F32, kind="ExternalOutput").ap()
    modscr_d = nc.dram_tensor("modscr", [L, NSEQ, D], F32, kind="Internal").ap()
    dbg_d = None
    if dbg is not None:
        dbg_d = nc.dram_tensor("dbg", list(dbg), F32, kind="ExternalOutput").ap()

    es = ExitStack()
    bufs = {}

    def sb(name, shape, dt=F32):
        t = es.enter_context(nc.sbuf_tensor(name, list(shape), dt))
        bufs[name] = Buf(name)
        return t

    def B(name):
        if name not in bufs:
            bufs[name] = Buf(name)
        return bufs[name]

    xres = sb("xres", [128, NT, D])
    hT = sb("hT", [128, 8, S], BF16)
    mixT = sb("mixT", [128, 8, S], BF16)
    NFM = 3
    fm = [sb(f"fm{i}", [128, S], BF16) for i in range(NFM)]
    vtok = sb("vtok", [128, NT, 128], BF16)
    NSTG = 2
    wstage = [sb(f"wstage{i}", [128, 8, 128]) for i in range(NSTG)]
    NWB = 2
    wbf = [sb(f"wbf{i}", [128, 8, 128], BF16) for i in range(NWB)]
    if S >= 1024:
        wout_bf = None
    else:
        wout_bf = sb("wout_bf", [128, 8, D], BF16)
    NF = 8
    ftb = sb("ftb", [128, NF, 512])
    ft = [ftb[:, i, :] for i in range(NF)]
    xs_ap = ftb[:, 6:8, :].rearrange("p a n -> p (a n)")
    iraw = sb("iraw", [128, S], BF16)
    itile = sb("itile", [128, 1, 512], I32)
    NBT = 4
    bt = [sb(f"bt{i}", [128, 512], BF16) for i in range(NBT)]
    bc = sb("bc", [128, D])
    ssmw = sb("ssmw", [128, 8, 4, 128], BF16)
    gluw_sb = sb("gluw_sb", [128, 2, 256], BF16)
    ident_f = sb("ident_f", [128, 128])
    ones_f = sb("ones_f", [128, 128])
    ident_b = sb("ident_b", [128, 128], BF16)
    ones_b = sb("ones_b", [128, 128], BF16)
    rowmask = sb("rowmask", [128, 8])
    psw = sb("psw", [128, 128])
    sgn = sb("sgn", [128, 1])
    pidx = sb("pidx", [128, 19])
    biasd = sb("biasd", [128, 4, 19])
    mask_d = sb("mask_d", [128, 128], BF16)
    swab = sb("swab", [128, 4, 256], BF16)
    condT = sb("condT", [128, 8, NSEQ])
    modT = sb("modT", [128, L, 24, NSEQ])
    adabT = sb("adabT", [128, L, 24])
    ngT = sb("ngT", [128, L, 8])
    gmodT = sb("gmodT", [128, L, 8, NSEQ])
    small = sb("small", [128, 64])
    sp_a = sb("sp_a", [128, 16, 16])
    sp_b = sb("sp_b", [128, 16, 16])
    sp_c = sb("sp_c", [128, 16, 16])
    sp_d = sb("sp_d", [128, 16, 16])
    sp_s = sb("sp_s", [128, 20, 16])
    ssmdT = sb("ssmdT", [128, L, 2])
    glubT = sb("glubT", [128, L, 2])
    sublnT = sb("sublnT", [128, L])
    sinkc = sb("sinkc", [128, L, 2])
    frq = sb("frq", [128, 16])
    rho = sb("rho", [128, 16])
    carry = sb("carry", [128, 16])
    psb = [es.enter_context(nc.psum_tensor(f"ps{i}", [128, 512], F32)) for i in range(8)]
    for i in range(8):
        bufs[f"ps{i}"] = Buf(f"ps{i}")

    SBUF_LEFT.append(nc.sbuf_bytes_remaining)
    def dma(out_ap, in_ap, reads, writes, slow=False):
        def fn(e):
            if slow:
                return e.dma_start(out=out_ap, in_=in_ap, allow_slow_non_contiguous=True)
            return e.dma_start(out=out_ap, in_=in_ap)
        return P.add("sp", fn, [B(r) for r in reads], [B(w) for w in writes])

    def op(eng, fn, reads, writes, gend=True):
        return P.add(eng, fn, [B(r) for r in reads], [B(w) for w in writes], gend)

    def mm(out_ap, lhsT, rhs, reads, writes, start=True, stop=True):
        return op("pe", lambda e: e.matmul(out_ap, lhsT, rhs, start=start, stop=stop),
                  reads, writes, gend=stop)

    def tr(out_ap, in_ap, ident, reads, writes, gend=True):
        return op("pe", lambda e: e.transpose(out_ap, in_ap, ident), reads, writes, gend=gend)

    def act(out_ap, in_ap, func, reads, writes, bias=None, scale=None, accum=None):
        kw = {}
        if bias is not None:
            kw["bias"] = bias
        if scale is not None:
            kw["scale"] = scale
        if accum is not None:
            kw["accum_out"] = accum
        return op("act", lambda e: e.activation(out_ap, in_ap, func, **kw), reads, writes)

    def ts(eng, out_ap, in_ap, s1, s2, op0, op1, reads, writes):
        if s2 is None:
            return op(eng, lambda e: e.tensor_scalar(out_ap, in_ap, s1, None, op0), reads, writes)
        return op(eng, lambda e: e.tensor_scalar(out_ap, in_ap, s1, s2, op0, op1), reads, writes)

    def tt(eng, out_ap, a, b, o, reads, writes):
        return op(eng, lambda e: e.tensor_tensor(out_ap, a, b, o), reads, writes)

    def stt(out_ap, in0, scalar, in1, op0, op1, reads, writes):
        return op("dve", lambda e: e.scalar_tensor_tensor(out_ap, in0, scalar, in1, op0, op1),
                  reads, writes)

    def cp(eng, out_ap, in_ap, reads, writes):
        if eng == "act":
            return op("act", lambda e: e.copy(out_ap, in_ap), reads, writes)
        return op(eng, lambda e: e.tensor_copy(out_ap, in_ap), reads, writes)

    def memset(eng, ap, val, writes):
        return op(eng, lambda e: e.memset(ap, val), [], writes)

    def ftview(i):
        return ftb[:, i:i + 2, :].rearrange("p a n -> p (a n)"), [f"ft{i}", f"ft{i + 1}"]

    SC2PI = 6.283185
    HALFPI = 1.5707963

    def pts(out_ap, in_ap, s1, reads, writes, s2=0.0, op0=ALU.mult, op1=ALU.add):
        return op("pool", lambda e: e.tensor_scalar(out_ap, in_ap, s1, s2, op0, op1), reads, writes)

    def consts():
        memset("pool", ones_f[:], 1.0, ["ones_f"])
        memset("pool", ones_b[:], 1.0, ["ones_b"])
        op("pool", lambda e: e.affine_select(ident_f[:], ones_f[:], [[-1, 128]], ALU.is_equal, 0.0,
                                             base=0, channel_multiplier=1), ["ones_f"], ["ident_f"])
        cp("pool", ident_b[:], ident_f[:], ["ident_f"], ["ident_b"])
        op("pool", lambda e: e.affine_select(rowmask[:], ones_f[:, 0:8], [[-16, 8]], ALU.is_ge, 0.0,
                                             base=0, channel_multiplier=1), ["ones_f"], ["rowmask"])
        op("pool", lambda e: e.affine_select(rowmask[:], rowmask[:], [[16, 8]], ALU.is_ge, 0.0,
                                             base=15, channel_multiplier=-1), ["rowmask"], ["rowmask"])
        op("pool", lambda e: e.affine_select(sgn[:], ones_f[:, 0:1], [[0, 1]], ALU.is_ge, -1.0,
                                             base=-64, channel_multiplier=1), ["ones_f"], ["sgn"])
        op("pool", lambda e: e.iota(pidx[:], [[-128, 19]], base=384, channel_multiplier=1,
                                    allow_small_or_imprecise_dtypes=True), [], ["pidx"])
        for h in range(4):
            pts(biasd[:, h, :], pidx[:], SLOPES[4 + h], ["pidx"], ["biasd"])
        memset("pool", ft[0][:, 0:128], 0.0, ["ft0"])
        op("pool", lambda e: e.affine_select(ft[0][:, 0:128], ft[0][:, 0:128], [[1, 128]], ALU.is_ge,
                                             -240000.0, base=0, channel_multiplier=-1), ["ft0"], ["ft0"])
        cp("pool", mask_d[:], ft[0][:, 0:128], ["ft0"], ["mask_d"])
        for h in range(4):
            sl8 = 8.0 * SLOPES[h]
            op("pool", lambda e: e.iota(ft[1][:, 0:128], [[1, 128]], base=128, channel_multiplier=-1,
                                        allow_small_or_imprecise_dtypes=True), [], ["ft1"])
            pts(ft[1][:, 0:128], ft[1][:, 0:128], -sl8, ["ft1"], ["ft1"])
            op("pool", lambda e: e.affine_select(ft[1][:, 0:128], ft[1][:, 0:128], [[-1, 128]], ALU.is_ge,
                                                 -240000.0, base=-1, channel_multiplier=1), ["ft1"], ["ft1"])
            cp("pool", swab[:, h, 0:128], ft[1][:, 0:128], ["ft1"], ["swab"])
            op("pool", lambda e: e.iota(ft[1][:, 128:256], [[1, 128]], base=0, channel_multiplier=-1,
                                        allow_small_or_imprecise_dtypes=True), [], ["ft1"])
            pts(ft[1][:, 128:256], ft[1][:, 128:256], -sl8, ["ft1"], ["ft1"])
            op("pool", lambda e: e.affine_select(ft[1][:, 128:256], ft[1][:, 128:256], [[1, 128]], ALU.is_ge,
                                                 -240000.0, base=0, channel_multiplier=-1), ["ft1"], ["ft1"])
            cp("pool", swab[:, h, 128:256], ft[1][:, 128:256], ["ft1"], ["swab"])
        memset("pool", ssmw[:], 0.0, ["ssmw2", "ssmw3"] + [f"ssmw{k}_{g}" for k in range(2) for g in range(8)])
        op("pool", lambda e: e.affine_select(psw[:], ones_f[:], [[-1, 128]], ALU.is_equal, 0.0,
                                             base=-64, channel_multiplier=1), ["ones_f"], ["psw"])
        op("pool", lambda e: e.affine_select(ft[2][:, 0:128], ones_f[:], [[-1, 128]], ALU.is_equal, 0.0,
                                             base=64, channel_multiplier=1), ["ones_f"], ["ft2"])
        tt("pool", psw[:], psw[:], ft[2][:, 0:128], ALU.add, ["psw", "ft2"], ["psw"])

    def small_loads():
        for b in range(NSEQ):
            dma(condT[:, :, b], c_d[b].rearrange("(j p) -> p j", p=128), [], ["condT"], slow=True)
        dma(adabT[:], adab_d.rearrange("l (j p) -> p l j", p=128), [], ["adabT"], slow=True)
        dma(ngT[:], ng_d.rearrange("l (j p) -> p l j", p=128), [], ["ngT"], slow=True)
        dma(ssmdT[:], ssmd_d.rearrange("l (j p) -> p l j", p=128), [], ["ssmdT"], slow=True)
        dma(glubT[:], glub_d.rearrange("l (j p) -> p l j", p=128), [], ["glubT"], slow=True)
        dma(sublnT[:], subln_d.rearrange("l p -> p l"), [], ["sublnT"], slow=True)
        for l in range(L):
            for kvg in range(2):
                for a in range(2):
                    dma(sinkc[a * 64:(a + 1) * 64, l, kvg:kvg + 1],
                        sinks_d[l, 2 * kvg + a:2 * kvg + a + 1].partition_broadcast(64),
                        [], ["sinkc"], slow=True)
        act(sinkc[:], sinkc[:], AF.Exp, ["sinkc"], ["sinkc"])
        act(condT[:], condT[:], AF.Silu, ["condT"], ["condT"])

    def ada_mod():
        stg = [(wstage[0][:].rearrange("p c n -> p (c n)"), ["wstage0"]),
               (wstage[1][:].rearrange("p c n -> p (c n)"), ["wstage1"]),
               ftview(0), ftview(2), ftview(4)]
        rows, rowsn = ftview(6)
        k = 0
        for l in range(L):
            for j in range(3):
                for c in range(8):
                    sv, sn = stg[k % len(stg)]
                    k += 1
                    dma(sv, adaw_d[l, c * 128:(c + 1) * 128, j * 1024:(j + 1) * 1024], [], sn)
                    for hf in range(2):
                        mm(psb[hf][0:NSEQ, :], condT[:, c, :], sv[:, hf * 512:(hf + 1) * 512],
                           sn + ["condT"], [f"ps{hf}"], start=(c == 0), stop=(c == 7))
                dma(rows[0:NSEQ, :], adab_d[l, j * 1024:(j + 1) * 1024].partition_broadcast(NSEQ), [], rowsn,
                    slow=True)
                for hf in range(2):
                    tt("dve", rows[0:NSEQ, hf * 512:(hf + 1) * 512], psb[hf][0:NSEQ, :],
                       rows[0:NSEQ, hf * 512:(hf + 1) * 512], ALU.add, [f"ps{hf}"] + rowsn, rowsn)
                if j == 2:
                    dma(modscr_d[l], rows[0:NSEQ, :], rowsn, ["modscr"])
                else:
                    for c in range(8):
                        tr(psb[2][:, c * NSEQ:(c + 1) * NSEQ], rows[0:NSEQ, c * 128:(c + 1) * 128],
                           ident_f[0:NSEQ, 0:NSEQ], rowsn + ["ident_f"], ["ps2"], gend=(c == 7))
                    cp("dve", modT[:, l, j * 8:(j + 1) * 8, :],
                       psb[2][:, 0:8 * NSEQ].rearrange("p (c b) -> p c b", c=8), ["ps2"], ["modT"])
            for j in range(8):
                ts("dve", gmodT[:, l, j, :], modT[:, l, 8 + j, :], 1.0, ngT[:, l, j:j + 1],
                   ALU.add, ALU.mult, ["modT", "ngT"], ["gmodT"])

    wstate = {"n": 0}

    def load_wtile(srcs):
        i = wstate["n"]
        wstate["n"] += 1
        st = i % NSTG
        wb = i % NWB
        for ap, c0, ncol in srcs:
            dma(wstage[st][:, :, c0:c0 + ncol], ap.rearrange("(c p) n -> p c n", p=128),
                [], [f"wstage{st}"])
        cp("pool", wbf[wb][:], wstage[st][:], [f"wstage{st}"], [f"wbf{wb}"])
        return wb

    evac_rr = {"n": 0}

    def rstd_col(src_col, dst_col, n):
        ts("dve", dst_col, src_col, 1.0 / n, EPS, ALU.mult, ALU.add, ["small"], ["small"])
        act(dst_col, dst_col, AF.Ln, ["small"], ["small"])
        act(dst_col, dst_col, AF.Exp, ["small"], ["small"], scale=-0.5)

    def norm_to_hT(l, b):
        junk, junkn = ftview(0)
        for t in range(NT):
            par = t % 2
            xsv, xsn = ftview(2 + 2 * par)
            sm = f"small{par}"
            col = small[:, 2 * par:2 * par + 1]
            col2 = small[:, 2 * par + 1:2 * par + 2]
            act(junk, xres[:, t, :], AF.Square, [f"xres{t}"], junkn + [sm], accum=col)
            ts("dve", col2, col, 1.0 / D, EPS, ALU.mult, ALU.add, [sm], [sm])
            act(col2, col2, AF.Ln, [sm], [sm])
            act(col2, col2, AF.Exp, [sm], [sm], scale=-0.5)
            ts("dve", xsv, xres[:, t, :], col2, None, ALU.mult, None, [f"xres{t}", sm], xsn)
            for half in range(2):
                pb = 4 + 2 * par + half
                for q in range(4):
                    c = half * 4 + q
                    tr(psb[pb][:, q * 128:(q + 1) * 128], xsv[:, c * 128:(c + 1) * 128], ident_f[:],
                       xsn + ["ident_f"], [f"ps{pb}"], gend=(q == 3))
                for q in range(4):
                    c = half * 4 + q
                    dst = hT[:, c, t * 128:(t + 1) * 128]
                    src = psb[pb][:, q * 128:(q + 1) * 128]
                    if half == 0:
                        ts("dve", dst, src, gmodT[:, l, c, b:b + 1], modT[:, l, c, b:b + 1],
                           ALU.mult, ALU.add, [f"ps{pb}", "gmodT", "modT"], [f"hT{c}"])
                    else:
                        act(dst, src, AF.Identity, [f"ps{pb}", "gmodT", "modT"], [f"hT{c}"],
                            bias=modT[:, l, c, b:b + 1], scale=gmodT[:, l, c, b:b + 1])
                    evac_rr["n"] += 1

    def evac_eng():
        evac_rr["n"] += 1
        return "dve" if evac_rr["n"] % 3 != 2 else "act"

    def proj_fm(wb, dst_tile, dst_name):
        eng = evac_eng()
        for tb in range(NB):
            pb = tb % 2
            for c in range(8):
                mm(psb[pb][:, :], wbf[wb][:, c, :], hT[:, c, tb * 512:(tb + 1) * 512],
                   [f"wbf{wb}", f"hT{c}"], [f"ps{pb}"], start=(c == 0), stop=(c == 7))
            cp(eng, dst_tile[:, tb * 512:(tb + 1) * 512], psb[pb][:, :], [f"ps{pb}"], [dst_name])

    def proj_tok(wb):
        eng = evac_eng()
        for t4 in range(NT // 4):
            pb = t4 % 2
            for q in range(4):
                t = t4 * 4 + q
                for c in range(8):
                    mm(psb[pb][:, q * 128:(q + 1) * 128], hT[:, c, t * 128:(t + 1) * 128], wbf[wb][:, c, :],
                       [f"wbf{wb}", f"hT{c}"], [f"ps{pb}"], start=(c == 0), stop=(c == 7))
            cp(eng, vtok[:, t4 * 4:(t4 + 1) * 4, :], psb[pb][:, :].rearrange("p (q n) -> p q n", q=4),
               [f"ps{pb}"], ["vtok"])

    def wcols(l, c0, n=128):
        return win_d[l, :, c0:c0 + n]

    def S(i):
        return sp_s[:, i, :]

    def bc16(ap):
        return ap.unsqueeze(2).to_broadcast([128, 16, 16])

    def v3(ap):
        return ap.rearrange("p (g h) -> p g h", g=16)

    def ssm_prep(l):
        R, W = ["sp_s"], ["sp_s"]
        zl = ft[0]
        dma(zl[0:16, 0:64], lamre_d[l], [], ["ft0"])
        dma(zl[0:16, 64:128], lamre_d[l], [], ["ft0"])
        dma(zl[0:16, 128:192], lamim_d[l], [], ["ft0"])
        dma(zl[0:16, 192:256], lamim_d[l], [], ["ft0"])
        tr(psb[0][:, 0:16], zl[0:16, 0:128], ident_f[0:16, 0:16], ["ft0", "ident_f"], ["ps0"])
        tr(psb[0][:, 16:32], zl[0:16, 128:256], ident_f[0:16, 0:16], ["ft0", "ident_f"], ["ps0"])
        cp("dve", S(0), psb[0][:, 0:16], ["ps0"], W)
        cp("dve", S(1), psb[0][:, 16:32], ["ps0"], W)
        dma(S(2), lstep_d[l].partition_broadcast(128), [], W, slow=True)
        act(S(3), S(2), AF.Exp, R, W)
        tt("dve", S(4), S(0), S(3), ALU.mult, R, W)
        tt("dve", S(5), S(1), S(3), ALU.mult, R, W)
        act(rho[:], S(4), AF.Exp, R, ["rho"])
        ts("dve", S(6), S(5), 1.0 / TWO_PI, None, ALU.mult, None, R, W)
        cp("dve", itile[:, 0, 0:16], S(6), R, ["itile"])
        tt("dve", frq[:], S(6), itile[:, 0, 0:16], ALU.subtract, R + ["itile"], ["frq"])
        ts("dve", itile[:, 0, 16:32], S(6), 0.25, None, ALU.add, None, R, ["itile"])
        tt("dve", S(7), S(6), itile[:, 0, 16:32], ALU.subtract, R + ["itile"], W)
        act(S(8), frq[:], AF.Sin, ["frq"], W, scale=SC2PI)
        act(S(9), S(7), AF.Sin, R, W, scale=SC2PI, bias=HALFPI)
        tt("dve", S(10), rho[:], S(9), ALU.mult, R + ["rho"], W)
        tt("dve", S(11), rho[:], S(8), ALU.mult, R + ["rho"], W)
        ts("dve", S(12), S(10), -1.0, None, ALU.add, None, R, W)
        tt("dve", S(13), S(0), S(0), ALU.mult, R, W)
        tt("dve", S(14), S(1), S(1), ALU.mult, R, W)
        tt("dve", S(13), S(13), S(14), ALU.add, R, W)
        op("dve", lambda e: e.reciprocal(S(13), S(13)), R, W)
        tt("dve", S(14), S(12), S(0), ALU.mult, R, W)
        tt("dve", S(15), S(11), S(1), ALU.mult, R, W)
        tt("dve", S(14), S(14), S(15), ALU.add, R, W)
        tt("dve", S(14), S(14), S(13), ALU.mult, R, W)
        tt("dve", S(15), S(11), S(0), ALU.mult, R, W)
        tt("dve", S(16), S(12), S(1), ALU.mult, R, W)
        tt("dve", S(15), S(15), S(16), ALU.subtract, R, W)
        tt("dve", S(15), S(15), S(13), ALU.mult, R, W)
        ts("dve", S(16), S(15), sgn[:, 0:1], None, ALU.mult, None, R + ["sgn"], W)
        ts("dve", S(17), S(14), sgn[:, 0:1], -1.0, ALU.mult, ALU.mult, R + ["sgn"], W)
        ts("dve", S(7), frq[:], 512.0, None, ALU.mult, None, ["frq"], W)
        cp("dve", itile[:, 0, 0:16], S(7), R, ["itile"])
        tt("dve", S(19), S(7), itile[:, 0, 0:16], ALU.subtract, R + ["itile"], W)
        ts("dve", itile[:, 0, 16:32], S(7), 0.25, None, ALU.add, None, R, ["itile"])
        tt("dve", S(18), S(7), itile[:, 0, 16:32], ALU.subtract, R + ["itile"], W)
        act(S(19), S(19), AF.Sin, R, W, scale=SC2PI)
        act(S(18), S(18), AF.Sin, R, W, scale=SC2PI, bias=HALFPI)
        ts("dve", S(19), S(19), sgn[:, 0:1], None, ALU.mult, None, R + ["sgn"], W)
        X1 = v3(ft[1][:, 0:256])
        X2 = v3(ft[2][:, 0:256])
        T = v3(ft[3][:, 0:256])
        dma(X1[0:64], bre_d[l].rearrange("g p h -> p g h"), [], ["ft1"], slow=True)
        dma(X1[64:128], bim_d[l].rearrange("g p h -> p g h"), [], ["ft1"], slow=True)
        dma(X2[0:64], bim_d[l].rearrange("g p h -> p g h"), [], ["ft2"], slow=True)
        dma(X2[64:128], bre_d[l].rearrange("g p h -> p g h"), [], ["ft2"], slow=True)
        tt("dve", T, X1, bc16(S(14)), ALU.mult, ["ft1", "sp_s"], ["ft3"])
        tt("dve", sp_a[:], X2, bc16(S(16)), ALU.mult, ["ft2", "sp_s"], ["sp_a"])
        tt("dve", sp_a[:], sp_a[:], T, ALU.add, ["sp_a", "ft3"], ["sp_a"])
        tt("dve", T, X2, bc16(S(17)), ALU.mult, ["ft2", "sp_s"], ["ft3"])
        tt("dve", sp_b[:], X1, bc16(S(15)), ALU.mult, ["ft1", "sp_s"], ["sp_b"])
        tt("dve", sp_b[:], sp_b[:], T, ALU.add, ["sp_b", "ft3"], ["sp_b"])
        for mc in range(2):
            Z1 = ft[4]
            Z2 = ft[5]
            cr = cre_d[l, mc * 8:(mc + 1) * 8].rearrange("g h p -> (g h) p")
            ci = cim_d[l, mc * 8:(mc + 1) * 8].rearrange("g h p -> (g h) p")
            dma(Z1[:, 0:64], cr, [], ["ft4"])
            dma(Z1[:, 64:128], ci, [], ["ft4"])
            dma(Z2[:, 0:64], ci, [], ["ft5"])
            dma(Z2[:, 64:128], cr, [], ["ft5"])
            tr(psb[1][:, 0:128], Z1[:, 0:128], ident_f[:], ["ft4", "ident_f"], ["ps1"])
            tr(psb[1][:, 128:256], Z2[:, 0:128], ident_f[:], ["ft5", "ident_f"], ["ps1"])
            cp("dve", sp_c[:, mc * 8:(mc + 1) * 8, :], psb[1][:, 0:128].rearrange("p (g h) -> p g h", g=8),
               ["ps1"], ["sp_c"])
            cp("dve", sp_d[:, mc * 8:(mc + 1) * 8, :], psb[1][:, 128:256].rearrange("p (g h) -> p g h", g=8),
               ["ps1"], ["sp_d"])
        gst = ft[3].rearrange("p (k n) -> p k n", k=2)
        dma(gst, gluw_d[l].rearrange("(k p) n -> p k n", p=128), [], ["ft3"])
        cp("pool", gluw_sb[:], gst, ["ft3"], ["gluw_sb"])

    def ssm_weights(mc):
        s1 = sp_a[:, mc * 8:(mc + 1) * 8, :].rearrange("p g h -> p (g h)")
        s2 = sp_b[:, mc * 8:(mc + 1) * 8, :].rearrange("p g h -> p (g h)")
        tr(psb[0][:, 0:128], s1, ident_f[:], ["sp_a", "ident_f"], ["ps0"])
        tr(psb[1][:, 128:256], s2, ident_f[:], ["sp_b", "ident_f"], ["ps1"])
        for gi in range(8):
            ts("dve", ssmw[:, gi, 0, :], psb[0][:, 0:128], rowmask[:, gi:gi + 1], None, ALU.mult, None,
               ["ps0", "rowmask"], [f"ssmw0_{gi}"])
            act(ssmw[:, gi, 1, :], psb[1][:, 128:256], AF.Identity, ["ps1", "rowmask"], [f"ssmw1_{gi}"],
                scale=rowmask[:, gi:gi + 1])
        d1 = bass.AP(ssmw, 2 * 128, [[8 * 4 * 128, 128], [4 * 128 + 16, 8], [1, 16]])
        d2 = bass.AP(ssmw, 3 * 128, [[8 * 4 * 128, 128], [4 * 128 + 16, 8], [1, 16]])
        ts("dve", d1, sp_c[:, mc * 8:(mc + 1) * 8, :], sgn[:, 0:1], -1.0, ALU.mult, ALU.mult,
           ["sp_c", "sgn"], ["ssmw2"])
        ts("dve", d2, sp_d[:, mc * 8:(mc + 1) * 8, :], -1.0, None, ALU.mult, None, ["sp_d"], ["ssmw3"])

    def ssm_phase(l, b):
        ssm_prep(l)
        for mc in range(2):
            wb = load_wtile([(wcols(l, mc * 128), 0, 128)])
            proj_fm(wb, fm[mc], f"fm{mc}")
            ssm_weights(mc)
            def gen_tables(gi):
                g = mc * 8 + gi
                k = gi % 2
                A, Bt = ft[k * 2], ft[k * 2 + 1]
                An, Bn = f"ft{k * 2}", f"ft{k * 2 + 1}"
                I0 = itile[:, 0, :]
                In = "itile"
                op("pool", lambda e, A=A: e.iota(A, [[1, 512]], base=0, channel_multiplier=0,
                                                 allow_small_or_imprecise_dtypes=True), [], [An])
                pts(A, A, frq[:, g:g + 1], [An, "frq"], [An])
                cp("pool", I0, A, [An], [In])
                tt("pool", Bt, A, I0, ALU.subtract, [An, In], [Bn])
                pts(I0, A, 0.25, [An], [In], s2=1.0, op0=ALU.add, op1=ALU.mult)
                tt("pool", A, A, I0, ALU.subtract, [An, In], [An])
                act(Bt, Bt, AF.Sin, [Bn], [Bn], scale=SC2PI)
                act(A, A, AF.Sin, [An], [An], scale=SC2PI, bias=HALFPI)

            gen_tables(0)
            iters = [(gi, tb) for gi in range(8) for tb in range(NB)]

            def bufs_of(i):
                gi, tb = iters[i]
                k = gi % 2
                kk = i % 2
                return dict(gi=gi, tb=tb, g=mc * 8 + gi, kk=kk,
                            A=ft[k * 2], Bt=ft[k * 2 + 1], An=f"ft{k * 2}", Bn=f"ft{k * 2 + 1}",
                            T1=ft[4 + kk], T2=ft[6 + kk], T1n=f"ft{4 + kk}", T2n=f"ft{6 + kk}",
                            P1=bt[kk * 2][:], P2=bt[kk * 2 + 1][:], P1n=f"bt{kk * 2}", P2n=f"bt{kk * 2 + 1}",
                            cols=slice(tb * 512, (tb + 1) * 512))

            def stage_bu(i):
                d = bufs_of(i)
                pa = d["kk"]
                mm(psb[pa][:, :], ssmw[:, d["gi"], 0, :], fm[mc][:, d["cols"]],
                   [f"ssmw0_{d['gi']}", f"fm{mc}"], [f"ps{pa}"])
                mm(psb[3][:, :], ssmw[:, d["gi"], 1, :], fm[mc][:, d["cols"]],
                   [f"ssmw1_{d['gi']}", f"fm{mc}"], ["ps3"])

            def stage_dve(i):
                d = bufs_of(i)
                pa, g, tb = d["kk"], d["g"], d["tb"]
                A, Bt, T1, T2, P1, P2 = d["A"], d["Bt"], d["T1"], d["T2"], d["P1"], d["P2"]
                An, Bn, T1n, T2n, P1n, P2n = d["An"], d["Bn"], d["T1n"], d["T2n"], d["P1n"], d["P2n"]
                tt("dve", T1, psb[pa][:, :], A, ALU.mult, [f"ps{pa}", An], [T1n])
                tt("dve", T2, psb[3][:, :], Bt, ALU.mult, ["ps3", Bn], [T2n])
                tt("dve", T1, T1, T2, ALU.add, [T1n, T2n], [T1n])
                if tb == 0:
                    init = 0.0
                else:
                    pk = (i - 1) % 2
                    wl, wln = ft[6 + pk][:, 511:512], f"ft{6 + pk}"
                    mm(psb[2][:, 0:1], psw[:], wl, ["psw", wln], ["ps2"])
                    ts("dve", small[:, 20:21], wl, sp_s[:, 18, g:g + 1], None, ALU.mult, None,
                       [wln, "sp_s"], ["small"])
                    stt(small[:, 21:22], psb[2][:, 0:1], sp_s[:, 19, g:g + 1], small[:, 20:21],
                        ALU.mult, ALU.add, ["ps2", "sp_s", "small"], ["small"])
                    init = small[:, 21:22]
                op("dve", lambda e, T2=T2, T1=T1, g=g, init=init: e.tensor_tensor_scan(
                    T2, rho[:, g:g + 1].to_broadcast([128, 512]), T1, init, ALU.mult, ALU.add),
                   [T1n, "rho", "small"], [T2n])
                tt("pool", P1, A, T2, ALU.mult, [An, T2n], [P1n])
                tt("dve", P2, Bt, T2, ALU.mult, [Bn, T2n], [P2n])

            def stage_y(i):
                d = bufs_of(i)
                gi, tb = d["gi"], d["tb"]
                mm(psb[4 + tb][:, :], ssmw[:, gi, 2, :], d["P1"], ["ssmw2", d["P1n"]], [f"ps{4 + tb}"],
                   start=(gi == 0), stop=False)
                mm(psb[4 + tb][:, :], ssmw[:, gi, 3, :], d["P2"], ["ssmw3", d["P2n"]], [f"ps{4 + tb}"],
                   start=False, stop=(gi == 7))

            for i in range(len(iters)):
                stage_bu(i)
                stage_dve(i)
                if i > 0:
                    stage_y(i - 1)
                gi, tb = iters[i]
                if tb == 0 and gi < 7:
                    gen_tables(gi + 1)
            stage_y(len(iters) - 1)
            for tb in range(NB):
                cols = slice(tb * 512, (tb + 1) * 512)
                Y, X2_ = ft[4 + tb % 2], ft[6 + tb % 2]
                Yn, Xn = f"ft{4 + tb % 2}", f"ft{6 + tb % 2}"
                stt(Y, fm[mc][:, cols], ssmdT[:, l, mc:mc + 1], psb[4 + tb][:, :], ALU.mult, ALU.add,
                    [f"fm{mc}", "ssmdT", f"ps{4 + tb}"], [Yn])
                act(X2_, Y, AF.Square, [Yn], [Xn])
                ts("dve", X2_, X2_, 0.044715, 1.0, ALU.mult, ALU.add, [Xn], [Xn])
                tt("dve", X2_, X2_, Y, ALU.mult, [Xn, Yn], [Xn])
                act(X2_, X2_, AF.Sigmoid, [Xn], [Xn], scale=1.5957691216)
                tt("dve", mixT[:, mc, cols], Y, X2_, ALU.mult, [Yn, Xn], ["mixT"])
        for nt in range(2):
            for tb in range(NB):
                cols = slice(tb * 512, (tb + 1) * 512)
                pb = tb % 2
                for k in range(2):
                    mm(psb[pb][:, :], gluw_sb[:, k, nt * 128:(nt + 1) * 128], mixT[:, k, cols],
                       ["gluw_sb", "mixT"], [f"ps{pb}"], start=(k == 0), stop=(k == 1))
                act(fm[1 + nt][:, cols], psb[pb][:, :], AF.Sigmoid, [f"ps{pb}", "glubT"], [f"fm{1 + nt}"],
                    bias=glubT[:, l, nt:nt + 1])
        for nt in range(2):
            wb = load_wtile([(wcols(l, 256 + nt * 128), 0, 128)])
            proj_fm(wb, fm[0], "fm0")
            act(fm[0][:, :], fm[0][:, :], AF.Silu, ["fm0"], ["fm0"])
            tt("dve", fm[1 + nt][:, :], fm[1 + nt][:, :], fm[0][:, :], ALU.mult, [f"fm{1 + nt}", "fm0"],
               [f"fm{1 + nt}"])
            tt("dve", mixT[:, nt, :], mixT[:, nt, :], fm[1 + nt][:, :], ALU.mult, ["mixT", f"fm{1 + nt}"],
               ["mixT"])

    NLAM, GCOL = small[:, 8:9], small[:, 9:10]

    def diff_prep(l):
        lam_init = 0.8 - 0.6 * math.exp(-0.3 * l)
        memset("pool", fm[1][64:128, :], 0.0, ["fm1"])
        memset("pool", iraw[0:64, :], 0.0, ["iraw"])
        lamb = ft[5][:, 0:256].rearrange("p (i n) -> p i n", i=4)
        for i, d_ in enumerate((lq1_d, lk1_d, lq2_d, lk2_d)):
            dma(lamb[:, i, :], d_[l].partition_broadcast(128), [], ["ft5"], slow=True)
        tt("dve", lamb[:, 0, :], lamb[:, 0, :], lamb[:, 1, :], ALU.mult, ["ft5"], ["ft5"])
        tt("dve", lamb[:, 2, :], lamb[:, 2, :], lamb[:, 3, :], ALU.mult, ["ft5"], ["ft5"])
        op("dve", lambda e: e.reduce_sum(small[:, 10:11], lamb[:, 0, :], AX.X), ["ft5"], ["small"])
        op("dve", lambda e: e.reduce_sum(small[:, 11:12], lamb[:, 2, :], AX.X), ["ft5"], ["small"])
        act(small[:, 10:12], small[:, 10:12], AF.Exp, ["small"], ["small"])
        tt("dve", NLAM, small[:, 11:12], small[:, 10:11], ALU.subtract, ["small"], ["small"])
        ts("dve", NLAM, NLAM, -lam_init, None, ALU.add, None, ["small"], ["small"])
        ts("dve", GCOL, sublnT[:, l:l + 1], 1.0 - lam_init, None, ALU.mult, None, ["sublnT"], ["small"])

    def diff_head(l, h):
        wq = load_wtile([(wcols(l, 512 + 128 * h), 0, 128)])
        proj_fm(wq, fm[0], "fm0")
        wk = load_wtile([(wcols(l, 1024 + 128 * h), 0, 128)])
        for tb in range(NB):
            pb = tb % 2
            for c in range(8):
                mm(psb[pb][:, :], wbf[wk][:, c, :], hT[:, c, tb * 512:(tb + 1) * 512],
                   [f"wbf{wk}", f"hT{c}"], [f"ps{pb}"], start=(c == 0), stop=(c == 7))
            cs_ = slice(tb * 512, (tb + 1) * 512)
            cp("dve", fm[1][0:64, cs_], psb[pb][0:64, :], [f"ps{pb}"], ["fm1"])
            cp("dve", iraw[64:128, cs_], psb[pb][64:128, :], [f"ps{pb}"], ["iraw"])
        wv = load_wtile([(wcols(l, 1536 + 128 * h), 0, 128)])
        proj_tok(wv)
        wz = load_wtile([(wcols(l, 2048 + 128 * h), 0, 128)])
        proj_fm(wz, fm[2], "fm2")
        act(fm[2][:, :], fm[2][:, :], AF.Silu, ["fm2"], ["fm2"])
        tiles = []
        for qc in range(NB):
            nkb = 4 * qc + 4
            for a in range(2):
                for kb in range(nkb):
                    tiles.append((qc, a, kb, nkb))
        LA = 2
        SB = (1, 2, 3)

        def emit_qk(i):
            qc, a, kb, nkb = tiles[i]
            rows = slice(a * 64, (a + 1) * 64)
            j = kb - 4 * qc
            c0 = 128 * j if j >= 0 else 0
            sb_ = SB[i % 3]
            kt = fm[1] if a == 0 else iraw
            mm(psb[sb_][:, c0:512], kt[:, kb * 128:(kb + 1) * 128],
               fm[0][:, qc * 512 + c0:(qc + 1) * 512], ["fm0", "fm1", "itile", "itile0", "itile1"], [f"ps{sb_}"],
               start=True, stop=(j < 0))
            if j >= 0:
                mm(psb[sb_][:, c0:c0 + 128], ident_b[:], mask_d[:], ["ident_b", "mask_d"], [f"ps{sb_}"],
                   start=False, stop=True)

        def emit_rest(i):
            qc, a, kb, nkb = tiles[i]
            j = kb - 4 * qc
            c0 = 128 * j if j >= 0 else 0
            sb_ = SB[i % 3]
            accb, sumb = 4 + a, 6 + a
            PT = bt[i % NBT]
            PTn = f"bt{i % NBT}"
            di = 4 * qc - kb + 3
            act(PT[:, c0:512], psb[sb_][:, c0:512], AF.Exp, [f"ps{sb_}", "biasd"], [PTn],
                bias=biasd[:, h, di:di + 1], scale=0.125)
            mm(psb[accb][:, c0:512], vtok[:, kb, :], PT[:, c0:512], ["vtok", PTn], [f"ps{accb}"],
               start=(kb == 0), stop=(kb == nkb - 1))
            mm(psb[sumb][:, c0:512], ones_b[:], PT[:, c0:512], ["ones_b", PTn], [f"ps{sumb}"],
               start=(kb == 0), stop=(kb == nkb - 1))

        def epilogue(qc):
            cols = slice(qc * 512, (qc + 1) * 512)
            r1, r2, o1, o2, sq = ft[0], ft[1], ft[2], ft[3], ft[4]
            act(r1, psb[6][:, :], AF.Ln, ["ps6"], ["ft0"])
            act(r2, psb[7][:, :], AF.Ln, ["ps7"], ["ft1"])
            cp("dve", o1, psb[4][:, :], ["ps4"], ["ft2"])
            cp("dve", o2, psb[5][:, :], ["ps5"], ["ft3"])
            steps = [
                lambda: act(r1, r1, AF.Exp, ["ft0"], ["ft0"], scale=-1.0),
                lambda: act(r2, r2, AF.Exp, ["ft1"], ["ft1"], scale=-1.0),
                lambda: tt("dve", o1, o1, r1, ALU.mult, ["ft2", "ft0"], ["ft2"]),
                lambda: tt("dve", o2, o2, r2, ALU.mult, ["ft3", "ft1"], ["ft3"]),
                lambda: stt(o1, o2, NLAM, o1, ALU.mult, ALU.add, ["ft3", "small", "ft2"], ["ft2"]),
                lambda: act(sq, o1, AF.Square, ["ft2"], ["ft4"]),
                lambda: None,
                lambda: mm(psb[0][:, :], ones_f[:], sq, ["ones_f", "ft4"], ["ps0"]),
                lambda: None,
                lambda: ts("dve", r1, psb[0][:, :], 1.0 / 128.0, EPS, ALU.mult, ALU.add, ["ps0"], ["ft0"]),
                lambda: act(r1, r1, AF.Ln, ["ft0"], ["ft0"]),
                lambda: act(r1, r1, AF.Exp, ["ft0"], ["ft0"], scale=-0.5),
                lambda: tt("dve", o1, o1, r1, ALU.mult, ["ft2", "ft0"], ["ft2"]),
                lambda: stt(mixT[:, 2 + h, cols], o1, GCOL, fm[2][:, cols], ALU.mult, ALU.mult,
                            ["ft2", "small", "fm2"], ["mixT"]),
            ]
            return steps

        n = len(tiles)
        pending = []
        for i in range(min(LA, n)):
            emit_qk(i)
        for i in range(n):
            emit_rest(i)
            if i + LA < n:
                emit_qk(i + LA)
            if pending:
                pending.pop(0)()
            qc, a, kb, nkb = tiles[i]
            if a == 1 and kb == nkb - 1:
                while pending:
                    pending.pop(0)()
                pending = epilogue(qc)
        while pending:
            pending.pop(0)()

    def swa_phase(l, b):
        wv = load_wtile([(wcols(l, 2944), 0, 128)])
        proj_tok(wv)
        pt_i = 0
        for kvg in range(2):
            wq = load_wtile([(wcols(l, 2560 + 128 * kvg), 0, 128)])
            proj_fm(wq, fm[0], "fm0")
            wk = load_wtile([(wcols(l, 2816 + 64 * kvg, 64), 0, 64), (wcols(l, 2816 + 64 * kvg, 64), 64, 64)])
            proj_fm(wk, fm[1], "fm1")
            wz = load_wtile([(wcols(l, 3072 + 128 * kvg), 0, 128)])
            proj_fm(wz, fm[2], "fm2")
            if kvg == 1 and "out" in PHASES:
                out_proj_load(l, b)
            act(fm[2][:, :], fm[2][:, :], AF.Silu, ["fm2"], ["fm2"])
            vc = slice(kvg * 64, (kvg + 1) * 64)

            def emit_qk(n):
                qcols = slice(n * 128, (n + 1) * 128)
                for a in range(2):
                    hh = 2 * kvg + a
                    rows = slice(a * 64, (a + 1) * 64)
                    sb_ = 2 * (n % 2) + a
                    sn = f"ps{sb_}"
                    if n > 0:
                        mm(psb[sb_][:, 0:128], fm[1][rows, (n - 1) * 128:n * 128], fm[0][rows, qcols],
                           ["fm0", "fm1"], [sn], start=True, stop=False)
                        mm(psb[sb_][:, 0:128], ident_b[:], swab[:, hh, 0:128], ["ident_b", "swab"],
                           [sn], start=False, stop=True)
                    mm(psb[sb_][:, 128:256], fm[1][rows, qcols], fm[0][rows, qcols],
                       ["fm0", "fm1"], [sn], start=True, stop=False)
                    mm(psb[sb_][:, 128:256], ident_b[:], swab[:, hh, 128:256], ["ident_b", "swab"],
                       [sn], start=False, stop=True)

            def emit_rest(n, PT, PTn):
                q = n % 4
                par = (n // 4) % 2
                accb, sumb = 4 + 2 * par, 5 + 2 * par
                oc = slice(q * 128, (q + 1) * 128)
                for a in range(2):
                    rows = slice(a * 64, (a + 1) * 64)
                    sb_ = 2 * (n % 2) + a
                    sn = f"ps{sb_}"
                    lo = 0 if n > 0 else 128
                    act(PT[:, a * 256 + lo:a * 256 + 256], psb[sb_][:, lo:256], AF.Exp,
                        [sn], [PTn], scale=0.125)
                    if n > 0:
                        mm(psb[accb][rows, oc], vtok[:, n - 1, vc], PT[:, a * 256:a * 256 + 128],
                           ["vtok", PTn], [f"ps{accb}"], start=True, stop=False)
                        mm(psb[sumb][rows, oc], ones_b[:, 0:64], PT[:, a * 256:a * 256 + 128],
                           ["ones_b", PTn], [f"ps{sumb}"], start=True, stop=False)
                    mm(psb[accb][rows, oc], vtok[:, n, vc], PT[:, a * 256 + 128:a * 256 + 256],
                       ["vtok", PTn], [f"ps{accb}"], start=(n == 0), stop=True)
                    mm(psb[sumb][rows, oc], ones_b[:, 0:64], PT[:, a * 256 + 128:a * 256 + 256],
                       ["ones_b", PTn], [f"ps{sumb}"], start=(n == 0), stop=True)

            def epilogue(n4):
                par = n4 % 2
                accb, sumb = 4 + 2 * par, 5 + 2 * par
                cols = slice(n4 * 512, (n4 + 1) * 512)
                den, o_ = ft[2 * par], ft[2 * par + 1]
                dn, on_ = f"ft{2 * par}", f"ft{2 * par + 1}"
                act(den, psb[sumb][:, :], AF.Ln, [f"ps{sumb}", "sinkc"], [dn], bias=sinkc[:, l, kvg:kvg + 1])
                cp("dve", o_, psb[accb][:, :], [f"ps{accb}"], [on_])
                return [
                    lambda: act(den, den, AF.Exp, [dn], [dn], scale=-1.0),
                    lambda: tt("dve", o_, o_, den, ALU.mult, [on_, dn], [on_]),
                    lambda: tt("dve", mixT[:, 6 + kvg, cols], o_, fm[2][:, cols], ALU.mult, [on_, "fm2"],
                               ["mixT"]),
                ]

            pending = []
            emit_qk(0)
            for n in range(NT):
                if n + 1 < NT:
                    emit_qk(n + 1)
                PT = bt[pt_i % NBT]
                PTn = f"bt{pt_i % NBT}"
                pt_i += 1
                emit_rest(n, PT, PTn)
                if pending:
                    pending.pop(0)()
                if n % 4 == 3:
                    while pending:
                        pending.pop(0)()
                    pending = epilogue(n // 4)
            while pending:
                pending.pop(0)()

    def wout_store():
        if wout_bf is None:
            return [hT[:, k, 0:D] for k in range(8)], [f"hT{k}" for k in range(8)]
        return [wout_bf[:, k, :] for k in range(8)], [f"wout_bf{k}" for k in range(8)]

    def out_proj_load(l, b):
        dma(bc[:], modscr_d[l, b:b + 1, :].partition_broadcast(128).rearrange("p o n -> p (o n)"),
            ["modscr"], ["bc"])
        wst, wname = wout_store()
        for k in range(8):
            i = wstate["n"]
            wstate["n"] += 1
            st = i % NSTG
            stv = wstage[st][:].rearrange("p c n -> p (c n)")
            dma(stv, wout_d[l, k * 128:(k + 1) * 128, :], [], [f"wstage{st}"])
            tt("pool", wst[k], stv, bc[:], ALU.mult, [f"wstage{st}", "bc"], [wname[k]])

    def out_proj(l, b):
        wst, wname = wout_store()
        for t in range(NT):
            for half in range(2):
                pb = (t * 2 + half) % 2
                cs = slice(half * 512, (half + 1) * 512)
                for k in range(8):
                    mm(psb[pb][:, :], mixT[:, k, t * 128:(t + 1) * 128], wst[k][:, cs], ["mixT", wname[k]],
                       [f"ps{pb}"], start=(k == 0), stop=(k == 7))
                tt("dve", xres[:, t, cs], psb[pb][:, :], xres[:, t, cs], ALU.add, [f"ps{pb}", f"xres{t}"], [f"xres{t}"])

    def final_norm(b):
        dma(bc[:], fg_d.partition_broadcast(128), [], ["bc"])
        junk, junkn = ftview(0)
        for t in range(NT):
            par = t % 3
            xsv, xsn = ftview(2 + 2 * par)
            sm = f"small{par}"
            col = small[:, 2 * par:2 * par + 1]
            col2 = small[:, 2 * par + 1:2 * par + 2]
            act(junk, xres[:, t, :], AF.Square, [f"xres{t}"], junkn + [sm], accum=col)
            ts("dve", col2, col, 1.0 / D, EPS, ALU.mult, ALU.add, [sm], [sm])
            act(col2, col2, AF.Ln, [sm], [sm])
            act(col2, col2, AF.Exp, [sm], [sm], scale=-0.5)
            stt(xsv, xres[:, t, :], col2, bc[:], ALU.mult, ALU.mult, [f"xres{t}", sm, "bc"], xsn)
            dma(out_d[b, t * 128:(t + 1) * 128, :], xsv, xsn, ["outd"])

    consts()
    small_loads()
    for t in range(NT):
        dma(xres[:, t, :], x_d[0, t * 128:(t + 1) * 128, :], [], [f"xres{t}"])
    ada_mod()
    phases = PHASES

    for b in range(NSEQ):
        if b > 0:
            for t in range(NT):
                dma(xres[:, t, :], x_d[b, t * 128:(t + 1) * 128, :], [], [f"xres{t}"])
        for l in range(L):
            norm_to_hT(l, b)
            if "ssm" in phases:
                ssm_phase(l, b)
            if "diff" in phases:
                diff_prep(l)
                for h in range(4):
                    diff_head(l, h)
            if "swa" in phases:
                swa_phase(l, b)
            if dbg is not None and l == 0 and b == 0:
                for c in range(8):
                    cp("dve", ft[2], mixT[:, c, 0:512], ["mixT"], ["ft2"])
                    dma(dbg_d[c], ft[2], ["ft2"], ["dbgout"])
            if "out" in phases:
                if "swa" not in phases:
                    out_proj_load(l, b)
                out_proj(l, b)
        final_norm(b)

    sems = {}
    for e in ("pe", "act", "dve", "pool"):
        sems[e] = es.enter_context(nc.semaphore(f"s_{e}"))
    dsems = [es.enter_context(nc.semaphore(f"s_dma{i}")) for i in range(Prog.NDMA)]
    P.resolve(sems, dsems)
    with nc.Block() as block:
        @block.sync
        def _(e):
            P.emit("sp", e, final=True)

        @block.tensor
        def _(e):
            P.emit("pe", e)

        @block.scalar
        def _(e):
            P.emit("act", e)

        @block.vector
        def _(e):
            P.emit("dve", e)

        @block.gpsimd
        def _(e):
            P.emit("pool", e)
    es.close()
    return nc


INPUT_NAMES = ["x", "c", "norm_gain", "ada_w", "ada_b", "w_in", "w_out", "ssm_lam_re", "ssm_lam_im",
               "ssm_log_step", "ssm_b_re", "ssm_b_im", "ssm_c_re", "ssm_c_im", "ssm_d", "glu_w", "glu_b",
               "diff_lq1", "diff_lk1", "diff_lq2", "diff_lk2", "diff_subln", "swa_sinks", "final_gain"]


def run(inputs, ncores, dbg=None):
    x = np.ascontiguousarray(np.asarray(inputs["x"], dtype=np.float32))
    Bt, S, _ = x.shape
    nseq = Bt // ncores
    nc = build_nc(nseq, S, dbg=dbg)
    in_maps = []
    for i in range(ncores):
        m = {}
        for k in INPUT_NAMES:
            a = np.ascontiguousarray(np.asarray(inputs[k], dtype=np.float32))
            if k in ("x", "c"):
                a = np.ascontiguousarray(a[i * nseq:(i + 1) * nseq])
            m[k] = a
        in_maps.append(m)
    res = run_bass_kernel_spmd(nc, in_maps, core_ids=list(range(ncores)))
    out = np.concatenate([np.asarray(r["out"]) for r in res.results], axis=0)
    if dbg is not None:
        return out, [np.asarray(r["dbg"]) for r in res.results]
    return out


def kernel(**inputs):
    return run(inputs, NCORES).astype(np.float32)
```
